# Optimizing a Trainium2 kernel written in Bass

```python
import jax, jax.numpy as jnp
from jax import lax
import numpy as np

D_MODEL = 1024
BATCH = 4
SEQ = 4096
DEPTH = 2

N_HEADS = 8
HEAD_DIM = 128
ATTN_WIDTH = N_HEADS * HEAD_DIM
IDX_HEADS = 16
IDX_DIM = 64
TOPK_MAX = 256
Q_BLOCK = 128
D_RNN = 1408
RNN_BLOCKS = 16
RNN_BLOCK_DIM = D_RNN // RNN_BLOCKS
CONV_WIDTH = 4
LRU_C = 8.0
ROPE_THETA = 10000.0
NORM_EPS = 1e-6

SPLIT_SIZES = (ATTN_WIDTH, ATTN_WIDTH, ATTN_WIDTH, ATTN_WIDTH,
               IDX_HEADS * IDX_DIM, IDX_DIM, IDX_HEADS,
               D_RNN, D_RNN, D_MODEL, D_MODEL)
N_IN = sum(SPLIT_SIZES)

kernel_name = "hybrid_dsa_rglru_gated_parallel"


def rmsnorm(x, g):
    xf = x.astype(jnp.float32)
    y = xf * lax.rsqrt(jnp.mean(xf * xf, axis=-1, keepdims=True) + NORM_EPS)
    return (y * g.astype(jnp.float32)).astype(x.dtype)


def rope_tables(positions, dim):
    inv = ROPE_THETA ** (-jnp.arange(0, dim, 2, dtype=jnp.float32) / dim)
    ang = positions.astype(jnp.float32)[..., None] * inv
    return jnp.cos(ang), jnp.sin(ang)


def apply_rope(x, cos, sin):
    extra = x.ndim - cos.ndim
    shape = cos.shape[:2] + (1,) * extra + cos.shape[2:]
    c, s = cos.reshape(shape), sin.reshape(shape)
    x1, x2 = jnp.split(x.astype(jnp.float32), 2, axis=-1)
    return jnp.concatenate([x1 * c - x2 * s, x2 * c + x1 * s], axis=-1).astype(x.dtype)


def dsa_attention(q, k, v, iq, ik, iw):
    B, S, H, Dh = q.shape
    topk = min(TOPK_MAX, S // 4)
    n_blk = S // Q_BLOCK
    key_pos = jnp.arange(S)
    scale = HEAD_DIM ** -0.5
    iw = iw.astype(jnp.float32) * (IDX_HEADS ** -0.5) * (IDX_DIM ** -0.5)

    def block(i):
        q0 = i * Q_BLOCK
        qb = lax.dynamic_slice_in_dim(q, q0, Q_BLOCK, axis=1)
        iqb = lax.dynamic_slice_in_dim(iq, q0, Q_BLOCK, axis=1)
        iwb = lax.dynamic_slice_in_dim(iw, q0, Q_BLOCK, axis=1)
        qpos = q0 + jnp.arange(Q_BLOCK)
        causal = key_pos[None, :] <= qpos[:, None]
        dots = jnp.einsum('bqhd,bsd->bqhs', iqb, ik, preferred_element_type=jnp.float32)
        idx_score = jnp.einsum('bqh,bqhs->bqs', iwb, jax.nn.relu(dots))
        idx_score = jnp.where(causal[None], idx_score, -jnp.inf)
        _, sel = lax.top_k(idx_score, topk)
        valid = sel <= qpos[None, :, None]
        k_sel = jax.vmap(lambda kb, ib: kb[ib])(k, sel)
        v_sel = jax.vmap(lambda vb, ib: vb[ib])(v, sel)
        logits = jnp.einsum('bqhd,bqkhd->bhqk', qb, k_sel, preferred_element_type=jnp.float32) * scale
        logits = jnp.where(valid[:, None], logits, -jnp.inf)
        p = jax.nn.softmax(logits, axis=-1)
        return jnp.einsum('bhqk,bqkhd->bqhd', p.astype(v.dtype), v_sel)

    out = lax.map(block, jnp.arange(n_blk))
    return out.transpose(1, 0, 2, 3, 4).reshape(B, S, H, Dh)


def causal_depthwise_conv(x, w, b):
    y = lax.conv_general_dilated(
        x, w[:, None, :].astype(x.dtype), window_strides=(1,),
        padding=[(CONV_WIDTH - 1, 0)], dimension_numbers=('NWC', 'WIO', 'NWC'),
        feature_group_count=x.shape[-1])
    return y + b.astype(x.dtype)


def rg_lru(x, w_r, b_r, w_i, b_i, lam):
    B, S, _ = x.shape
    xb = x.reshape(B, S, RNN_BLOCKS, RNN_BLOCK_DIM)
    r = jax.nn.sigmoid((jnp.einsum('bsnc,ncd->bsnd', xb, w_r).reshape(B, S, D_RNN) + b_r).astype(jnp.float32))
    i = jax.nn.sigmoid((jnp.einsum('bsnc,ncd->bsnd', xb, w_i).reshape(B, S, D_RNN) + b_i).astype(jnp.float32))
    log_a = -LRU_C * r * jax.nn.softplus(-lam.astype(jnp.float32))
    a = jnp.exp(log_a)
    u = jnp.sqrt(-jnp.expm1(2.0 * log_a)) * (i * x.astype(jnp.float32))

    def combine(left, right):
        a_l, b_l = left
        a_r, b_r = right
        return a_l * a_r, a_r * b_l + b_r

    _, h = lax.associative_scan(combine, (a, u), axis=1)
    return h.astype(x.dtype)


def setup_inputs(seed: int = 0) -> dict:
    key = jax.random.key(seed)
    ks = jax.random.split(key, 16)
    f32 = jnp.float32
    x = jax.random.normal(ks[0], (BATCH, SEQ, D_MODEL), f32)
    positions = jnp.broadcast_to(jnp.arange(SEQ, dtype=jnp.int32), (BATCH, SEQ))
    norm_g = 1.0 + 0.02 * jax.random.normal(ks[1], (DEPTH, D_MODEL), f32)
    w_in = jax.random.normal(ks[2], (DEPTH, D_MODEL, N_IN), f32) * D_MODEL ** -0.5
    conv_w = jax.random.normal(ks[3], (DEPTH, CONV_WIDTH, D_RNN), f32) * CONV_WIDTH ** -0.5
    conv_b = 0.01 * jax.random.normal(ks[4], (DEPTH, D_RNN), f32)
    w_rg = jax.random.normal(ks[5], (DEPTH, RNN_BLOCKS, RNN_BLOCK_DIM, RNN_BLOCK_DIM), f32) * RNN_BLOCK_DIM ** -0.5
    b_rg = 0.01 * jax.random.normal(ks[6], (DEPTH, D_RNN), f32)
    w_ig = jax.random.normal(ks[7], (DEPTH, RNN_BLOCKS, RNN_BLOCK_DIM, RNN_BLOCK_DIM), f32) * RNN_BLOCK_DIM ** -0.5
    b_ig = 0.01 * jax.random.normal(ks[8], (DEPTH, D_RNN), f32)
    a0 = jax.random.uniform(ks[9], (DEPTH, D_RNN), f32, 0.9, 0.999)
    p = a0 ** (1.0 / LRU_C)
    lru_lambda = jnp.log(p) - jnp.log1p(-p)
    w_out_attn = jax.random.normal(ks[10], (DEPTH, ATTN_WIDTH, D_MODEL), f32) * ATTN_WIDTH ** -0.5
    w_out_rnn = jax.random.normal(ks[11], (DEPTH, D_RNN, D_MODEL), f32) * D_RNN ** -0.5
    w_o = jax.random.normal(ks[12], (DEPTH, D_MODEL, D_MODEL), f32) * D_MODEL ** -0.5
    final_g = 1.0 + 0.02 * jax.random.normal(ks[13], (D_MODEL,), f32)
    return {"x": x, "positions": positions, "norm_g": norm_g, "w_in": w_in,
            "conv_w": conv_w, "conv_b": conv_b, "w_rg": w_rg, "b_rg": b_rg,
            "w_ig": w_ig, "b_ig": b_ig, "lru_lambda": lru_lambda,
            "w_out_attn": w_out_attn, "w_out_rnn": w_out_rnn, "w_o": w_o,
            "final_g": final_g}


def reference(x, positions, norm_g, w_in, conv_w, conv_b, w_rg, b_rg, w_ig, b_ig,
              lru_lambda, w_out_attn, w_out_rnn, w_o, final_g):
    B, S, _ = x.shape
    offsets = np.cumsum(SPLIT_SIZES)[:-1].tolist()
    cos_a, sin_a = rope_tables(positions, HEAD_DIM)
    cos_i, sin_i = rope_tables(positions, IDX_DIM)
    for l in range(DEPTH):
        h = rmsnorm(x, norm_g[l])
        proj = h @ w_in[l]
        q, k, v, ga, iq, ik, iw, xr, gr, ma, mb = jnp.split(proj, offsets, axis=-1)
        q = apply_rope(q.reshape(B, S, N_HEADS, HEAD_DIM), cos_a, sin_a)
        k = apply_rope(k.reshape(B, S, N_HEADS, HEAD_DIM), cos_a, sin_a)
        v = v.reshape(B, S, N_HEADS, HEAD_DIM)
        iq = apply_rope(iq.reshape(B, S, IDX_HEADS, IDX_DIM), cos_i, sin_i)
        ik = apply_rope(ik, cos_i, sin_i)
        attn = dsa_attention(q, k, v, iq, ik, iw).reshape(B, S, ATTN_WIDTH)
        y_a = (attn * jax.nn.silu(ga)) @ w_out_attn[l]
        xr = causal_depthwise_conv(xr, conv_w[l], conv_b[l])
        hr = rg_lru(xr, w_rg[l], b_rg[l], w_ig[l], b_ig[l], lru_lambda[l])
        y_b = (hr * jax.nn.silu(gr)) @ w_out_rnn[l]
        merged = jax.nn.sigmoid(ma) * y_a + jax.nn.sigmoid(mb) * y_b
        x = x + merged @ w_o[l]
    return rmsnorm(x, final_g)
```

```python
import numpy as np
from contextlib import ExitStack
import ml_dtypes
import concourse.bass as bass
import concourse.mybir as mybir
from concourse.bass_utils import run_bass_kernel_spmd

F32 = mybir.dt.float32
BF = mybir.dt.bfloat16
I32 = mybir.dt.int32
AF = mybir.ActivationFunctionType
ALU = mybir.AluOpType
AX = mybir.AxisListType

T = 4096
D = 1024
NT = 32
DEPTH = 2
NIN = 10064
O_Q, O_K, O_V, O_GA, O_IQ, O_IK, O_IW, O_XR, O_GR, O_MA, O_MB = (
    0, 1024, 2048, 3072, 4096, 5120, 5184, 5200, 6608, 8016, 9040)
DR = 1408
NB = 16
BD = 88
TOPK = 256
NIT = 16
EPS = 1e-6
NEG = -30000.0
ENGS = ("pe", "act", "dve", "pool", "sp")


class Buf:
    __slots__ = ("name", "w", "r", "closed", "sem")

    def __init__(self, name="b"):
        self.name = name
        self.w = {}
        self.r = {}
        self.closed = False
        self.sem = None


class Prog:
    def __init__(self, nc):
        self.nc = nc
        self.q = {e: [] for e in ENGS}
        self.cnt = {e: 0 for e in ENGS}
        self.waited = {e: {} for e in ENGS}
        self.nsem_dma = 0
        self.key = {e: e for e in ENGS}
        self.epoch = 0
        self.free_dma = []
        self.live_dma = []

    def new_epoch(self):
        self.barrier()
        self.epoch += 1
        for e in ENGS:
            self.key[e] = "%s_%d" % (e, self.epoch)
            self.cnt[self.key[e]] = 0

    def _collect(self, reads, writes):
        deps = {}

        def add(d):
            for k, v in d.items():
                if deps.get(k, 0) < v:
                    deps[k] = v

        for b in reads:
            add(b.w)
        for b, dj in writes:
            if dj and not b.closed:
                continue
            add(b.w)
            add(b.r)
        return deps

    def _commit(self, ev, reads, writes):
        k, v = ev
        for b, dj in writes:
            if dj and not b.closed:
                if b.w.get(k, 0) < v:
                    b.w[k] = v
            else:
                b.w = {k: v}
                b.r = {}
                b.closed = False
        for b in reads:
            if b.r.get(k, 0) < v:
                b.r[k] = v
            b.closed = True

    def _emit_waits(self, eng, deps):
        wt = self.waited[eng]
        for k, v in deps.items():
            if wt.get(k, 0) < v:
                self.q[eng].append(("wait", k, v))
                wt[k] = v

    @staticmethod
    def _nw(writes):
        return [w if isinstance(w, tuple) else (w, False) for w in writes]

    def op(self, eng, fn, reads=(), writes=()):
        writes = self._nw(writes)
        self._emit_waits(eng, self._collect(reads, writes))
        k = self.key[eng]
        self.cnt[k] += 1
        self.q[eng].append(("op", fn, k, 1))
        self._commit((k, self.cnt[k]), reads, writes)

    def dma(self, eng, fn, reads, writes, slot):
        writes = self._nw(writes)
        self._emit_waits(eng, self._collect(reads, writes))
        if slot.sem is None:
            if self.free_dma:
                slot.sem = self.free_dma.pop()
            else:
                slot.sem = "d%d" % self.nsem_dma
                self.nsem_dma += 1
                self.cnt[slot.sem] = 0
            self.live_dma.append(slot)
        self.cnt[slot.sem] += 16
        self.q[eng].append(("op", fn, slot.sem, 16))
        self._commit((slot.sem, self.cnt[slot.sem]), reads, writes)

    def barrier(self):
        for e in ENGS:
            self._emit_waits(e, dict(self.cnt))
        for b in self.live_dma:
            self.free_dma.append(b.sem)
            b.sem = None
        self.live_dma = []

    def emit(self, stack):
        nc = self.nc
        sems = {k: stack.enter_context(nc.semaphore("s_" + k)) for k in self.cnt}
        block = stack.enter_context(nc.Block())
        engmap = {"pe": "tensor", "act": "scalar", "dve": "vector", "pool": "gpsimd", "sp": "sync"}

        def make(e):
            def body(engobj):
                for item in self.q[e]:
                    if item[0] == "wait":
                        engobj.wait_ge(sems[item[1]], item[2])
                    else:
                        item[1](engobj).then_inc(sems[item[2]], item[3])
            return body

        for e in ENGS:
            getattr(block, engmap[e])(make(e))


class K:
    def __init__(self, debug=None, nlayers=DEPTH, stop=None, phases=None, lim=None, ext=()):
        self.stop = stop
        self.phases = phases or ["0", "A", "B", "C", "D1", "D2", "E"]
        self.lim = lim or {}
        self.ext = ext
        self.debug = debug
        self.nlayers = nlayers
        self.nc = bass.Bass("TRN2", target_bir_lowering=False)
        self.st = ExitStack()

    def act(self, out, in_, func, r, w, **kw):
        self.P.op("act", lambda e: e.activation(out=out, in_=in_, func=func, **kw), r, w)

    def ts(self, eng, out, in0, s1, s2, op0, op1, r, w, **kw):
        if op1 is None:
            self.P.op(eng, lambda e: e.tensor_scalar(out=out, in0=in0, scalar1=s1, scalar2=None, op0=op0, **kw), r, w)
        else:
            self.P.op(eng, lambda e: e.tensor_scalar(out=out, in0=in0, scalar1=s1, scalar2=s2, op0=op0, op1=op1, **kw), r, w)

    def tt(self, eng, out, in0, in1, op, r, w):
        self.P.op(eng, lambda e: e.tensor_tensor(out=out, in0=in0, in1=in1, op=op), r, w)

    def stt(self, out, in0, scalar, in1, op0, op1, r, w):
        self.P.op("dve", lambda e: e.scalar_tensor_tensor(out=out, in0=in0, scalar=scalar, in1=in1, op0=op0, op1=op1), r, w)

    def cp(self, eng, out, in_, r, w):
        if eng == "act":
            self.P.op("act", lambda e: e.activation(out=out, in_=in_, func=AF.Copy), r, w)
        else:
            self.P.op(eng, lambda e: e.tensor_copy(out=out, in_=in_), r, w)

    def mm(self, out, lhsT, rhs, start, stop, r, w):
        w = [(b, True) for b in w]
        self.P.op("pe", lambda e: e.matmul(out, lhsT=lhsT, rhs=rhs, start=start, stop=stop), r, w)

    def tr(self, out, in_, ident, r, w):
        w = [(b, True) for b in w]
        self.P.op("pe", lambda e: e.transpose(out=out, in_=in_, identity=ident), r, w)

    def ld(self, out, in_, slot, r=(), eng="sp"):
        self.P.dma(eng, lambda e: e.dma_start(out=out, in_=in_), list(r), [slot], slot)

    def stor(self, out, in_, slot, dbuf, eng="pool"):
        self.P.dma(eng, lambda e: e.dma_start(out=out, in_=in_), [slot], [(dbuf, True)], slot)

    def arena_reset(self, base=None):
        self.aoff = self.abase if base is None else base

    def al(self, words, dtype=F32, parts=128):
        assert self.aoff + words <= self.AW, ("arena overflow", self.aoff, words, self.AW)
        v = self.arena[0:parts, self.aoff:self.aoff + words]
        self.aoff += words
        if dtype != F32:
            v = v.bitcast(dtype)
        return v

    def alb(self, n_bf16, parts=128):
        return self.al((n_bf16 + 1) // 2, BF, parts)

    def build(self):
        nc, st = self.nc, self.st
        dt = lambda n, s, d, k: nc.dram_tensor(n, s, d, kind=k).ap()
        I = {}
        I["x"] = dt("x", [T, D], F32, "ExternalInput")
        I["pos"] = dt("pos", [1, T], I32, "ExternalInput")
        I["norm_g"] = dt("norm_g", [DEPTH, 128, 8], F32, "ExternalInput")
        I["w_in"] = dt("w_in", [DEPTH, D, NIN], F32, "ExternalInput")
        I["convw"] = dt("convw", [DEPTH, BD, NB * 4], F32, "ExternalInput")
        I["chv"] = dt("chv", [DEPTH, BD, 4 * NB], F32, "ExternalInput")
        I["w_rg"] = dt("w_rg", [DEPTH, NB, BD, BD], F32, "ExternalInput")
        I["w_ig"] = dt("w_ig", [DEPTH, NB, BD, BD], F32, "ExternalInput")
        I["w_oa"] = dt("w_oa", [DEPTH, D, D], F32, "ExternalInput")
        I["w_or"] = dt("w_or", [DEPTH, DR, D], F32, "ExternalInput")
        I["w_o"] = dt("w_o", [DEPTH, D, D], F32, "ExternalInput")
        I["final_g"] = dt("final_g", [1, D], F32, "ExternalInput")
        I["cbf"] = dt("cbf", [128, 4 * 128], BF, "ExternalInput")
        I["cf"] = dt("cf", [128, 128 + 2 + NIT + 2], F32, "ExternalInput")
        out = dt("out", [T, D], F32, "ExternalOutput")
        self.I = I
        S = {}
        _dt = dt
        dt = lambda n, s_, d, k: _dt(n, s_, d, "ExternalInput" if (k == "Internal" and n[2:] in self.ext) else k)
        S["tabs"] = dt("s_tabs", [4, 128, T], F32, "Internal")
        S["x1"] = dt("s_x1", [T, D], F32, "Internal")
        S["qT"] = dt("s_qT", [8, 128, T], BF, "Internal")
        S["kT"] = dt("s_kT", [8, 128, T], BF, "Internal")
        S["iqT"] = dt("s_iqT", [8, 128, T], BF, "Internal")
        S["ikT"] = dt("s_ikT", [128, T], BF, "Internal")
        S["v"] = dt("s_v", [T, D], BF, "Internal")
        S["sgaT"] = dt("s_sgaT", [8, 128, T], BF, "Internal")
        S["iw"] = dt("s_iw", [T, 16], F32, "Internal")
        S["smaT"] = dt("s_smaT", [8, 128, T], F32, "Internal")
        S["smbT"] = dt("s_smbT", [8, 128, T], F32, "Internal")
        S["ybT"] = dt("s_ybT", [NB, BD, T], BF, "Internal")
        S["gaoT"] = dt("s_gaoT", [8, 128, T], BF, "Internal")
        S["maskT"] = dt("s_maskT", [8, 128, 32 * 512], BF, "Internal")
        self.S = S
        dbg = {}
        if self.debug:
            for name in self.debug:
                src = S[name]
                dbg[name] = dt("dbg_" + name, list(src.shape), src.dtype, "ExternalOutput")
        self.dbg = dbg

        with st:
            self.AW = 47 * 1024
            self.arena = st.enter_context(nc.sbuf_tensor("arena", [128, self.AW], F32))
            self.pb = [st.enter_context(nc.psum_tensor("pb%d" % i, [128, 512], F32)) for i in range(8)]
            self.PB = [Buf("pb%d" % i) for i in range(8)]
            self.P = Prog(nc)
            P = self.P
            self.dram = Buf("dram")

            self.aoff = 0
            cb = self.alb(512)
            cf = self.al(128 + 2 + NIT + 2)
            self.abase = self.aoff
            self.Bc = Buf("const")
            self.ld(cb, I["cbf"][:, :], self.Bc)
            self.Bc2 = Buf("const2")
            self.ld(cf, I["cf"][:, :], self.Bc2)
            self.ident = cb[:, 0:128]
            self.Rattn = cb[:, 128:256]
            self.Ridx = cb[:, 256:384]
            self.ones = cb[:, 384:512]
            self.causal = cf[:, 0:128]
            self.invA = cf[:, 128:129]
            self.invI = cf[:, 129:130]
            self.pow2 = cf[:, 130:130 + NIT + 1]
            self.CB = [self.Bc, self.Bc2]

            ph = self.phases
            if "0" in ph:
                self.phase0()
            for l in range(self.nlayers):
                xin = I["x"] if l == 0 else S["x1"]
                last = (l == self.nlayers - 1)
                if "A" in ph:
                    self.phaseAB(l, xin)
                if "C" in ph:
                    self.phaseC(l)
                if "D1" in ph:
                    self.phaseD1(l)
                if "D2" in ph:
                    self.phaseD2(l)
                if "E" in ph:
                    self.phaseE(l, xin, out if last else S["x1"], last)
            P.barrier()
            for name, dst in dbg.items():
                b = Buf("dbg" + name)
                src = S[name]
                if len(src.shape) == 3:
                    for i in range(src.shape[0]):
                        self.P.dma("sp", (lambda s_, d_: (lambda e: e.dma_start(out=d_, in_=s_)))(src[i], dst[i]), [], [b], b)
                else:
                    self.P.dma("sp", (lambda s_, d_: (lambda e: e.dma_start(out=d_, in_=s_)))(src, dst), [], [b], b)
            P.barrier()
            P.emit(st)
        return nc

    def phase0(self):
        P, I, S = self.P, self.I, self.S
        self.arena_reset()
        posi = self.al(T, I32)
        posf = self.al(T)
        ang = self.al(T)
        t1 = self.al(T)
        t2 = self.al(T)
        t3 = self.al(T)
        Bp, Bf_, Ba, B1, B2, B3 = (Buf(n) for n in ("posi", "posf", "ang", "t1", "t2", "t3"))
        self.ld(posi, I["pos"][0:1, :].broadcast_to([128, T]), Bp)
        self.cp("dve", posf, posi, [Bp], [Bf_])
        TWO_PI = 2.0 * np.pi
        C1 = 6.28125
        C2 = TWO_PI - C1
        PI = float(np.pi)
        for ti, inv in enumerate((self.invA, self.invI)):
            self.ts("dve", ang, posf, inv, None, ALU.mult, None, [Bf_] + self.CB, [Ba])
            self.ts("dve", t1, ang, 1.0 / TWO_PI, None, ALU.mult, None, [Ba], [B1])
            ki = t2.bitcast(I32)
            self.cp("dve", ki, t1, [B1], [B2])
            self.cp("dve", t1, ki, [B2], [B1])
            self.stt(t2, t1, -C1, ang, ALU.mult, ALU.add, [B1, Ba], [B2])
            self.stt(t2, t1, -C2, t2, ALU.mult, ALU.add, [B1, B2], [B2])
            for which in (0, 1):
                if which == 0:
                    self.ts("dve", t3, t2, PI / 2, None, ALU.add, None, [B2], [B3])
                    src = t3
                    Bs = B3
                else:
                    src = t2
                    Bs = B2
                self.ts("dve", t1, src, PI, -TWO_PI, ALU.is_gt, ALU.mult, [Bs], [B1])
                self.tt("dve", t3, src, t1, ALU.add, [Bs, B1], [B3])
                self.ts("dve", t3, t3, -PI, PI, ALU.max, ALU.min, [B3], [B3])
                self.act(t3, t3, AF.Sin, [B3], [B3])
                self.stor(S["tabs"][2 * ti + which], t3, B3, self.dram)
        P.barrier()

    def phaseAB(self, l, xin):
        P, I, S = self.P, self.I, self.S
        P.new_epoch()
        self.arena_reset()
        hT = self.alb(8 * T).rearrange("p (c t) -> p c t", c=8)
        BhT = [Buf("hT%d" % i) for i in range(8)]
        self.hT, self.BhT = hT, BhT
        base_after_hT = self.aoff
        g = self.al(8)
        Bg = Buf("g")
        self.ld(g, I["norm_g"][l], Bg)
        xs = [self.al(D) for _ in range(4)]
        Bx = [Buf("x%d" % i) for i in range(4)]
        xn = [self.alb(D) for _ in range(4)]
        Bxn = [Buf("xn%d" % i) for i in range(4)]
        junk = self.al(D)
        Bj = Buf("junk")
        sm = self.al(8)
        Bs_ = Buf("ss")
        for tt_ in range(NT):
            s = tt_ % 4
            self.ld(xs[s], xin[tt_ * 128:(tt_ + 1) * 128, :], Bx[s])
            ss = sm[:, 0:1]
            self.act(junk, xs[s], AF.Square, [Bx[s]], [Bj, Bs_], accum_out=ss)
            self.ts("dve", sm[:, 1:2], ss, 1.0 / D, EPS, ALU.mult, ALU.add, [Bs_], [Bs_])
            self.act(sm[:, 2:3], sm[:, 1:2], AF.Sqrt, [Bs_], [Bs_])
            self.P.op("dve", (lambda o, i_: (lambda e: e.reciprocal(out=o, in_=i_)))(sm[:, 3:4], sm[:, 2:3]), [Bs_], [Bs_])
            self.act(xn[s], xs[s], AF.Copy, [Bx[s], Bs_], [Bxn[s]], scale=sm[:, 3:4])
            for half in range(2):
                pbi = (tt_ * 2 + half) % 2
                pv = self.pb[pbi][:].bitcast(BF)
                for c4 in range(4):
                    c = half * 4 + c4
                    self.tr(pv[:, c4 * 128:(c4 + 1) * 128], xn[s][:, c * 128:(c + 1) * 128], self.ident,
                            [Bxn[s], self.Bc], [self.PB[pbi]])
                for c4 in range(4):
                    c = half * 4 + c4
                    self.ts("dve", hT[:, c, tt_ * 128:(tt_ + 1) * 128], pv[:, c4 * 128:(c4 + 1) * 128],
                            g[:, c:c + 1], None, ALU.mult, None, [self.PB[pbi], Bg], [(BhT[tt_ // 4], True)])
        if self.stop == "A":
            dh = self.nc.dram_tensor("dbg_hT", [128, 8 * T], BF, kind="ExternalOutput").ap()
            bb = Buf("dbghT")
            self.P.dma("sp", lambda e: e.dma_start(out=dh, in_=hT.rearrange("p c t -> p (c t)")), BhT, [bb], bb)
            self.base_after_hT = base_after_hT
            return
        if "B" not in self.phases:
            self.base_after_hT = base_after_hT
            return
        P.barrier()
        self.arena_reset(base_after_hT)
        tabc = self.al(T)
        tabs_ = self.al(T)
        Btab = Buf("tab")
        wst = [self.al(8 * 256).rearrange("p (c n) -> p c n", c=8) for _ in range(2)]
        Bwst = [Buf("wst%d" % i) for i in range(2)]
        wb = [self.alb(8 * 256).rearrange("p (c n) -> p c n", c=8) for _ in range(2)]
        Bwb = [Buf("wb%d" % i) for i in range(2)]
        qb_ = [self.alb(512) for _ in range(2)]
        Bqb = [Buf("qb%d" % i) for i in range(2)]
        t1 = [self.al(512) for _ in range(2)]
        Bt1 = [Buf("t1%d" % i) for i in range(2)]
        t2 = [self.al(512) for _ in range(2)]
        Bt2 = [Buf("t2%d" % i) for i in range(2)]
        ob = [self.alb(512) for _ in range(6)]
        Bob = [Buf("ob%d" % i) for i in range(6)]
        of = [self.al(512) for _ in range(6)]
        Bof = [Buf("of%d" % i) for i in range(6)]
        self.cnt_w = 0
        self.cnt_e = 0
        self.cnt_o = 0
        self.cnt_f = 0
        w_in = I["w_in"]

        def load_w(col0, ncols, dup=False):
            s = self.cnt_w % 2
            self.cnt_w += 1
            if dup:
                for hh in range(2):
                    self.ld(wst[s][:, :, hh * 64:(hh + 1) * 64],
                            w_in[l, :, col0:col0 + 64].rearrange("(c p) n -> p c n", p=128), Bwst[s])
                ncols = 128
            else:
                self.ld(wst[s][:, :, 0:ncols], w_in[l, :, col0:col0 + ncols].rearrange("(c p) n -> p c n", p=128), Bwst[s])
            self.cp("act", wb[s][:, :, 0:ncols], wst[s][:, :, 0:ncols], [Bwst[s]], [Bwb[s]])
            return wb[s], Bwb[s]

        def fm_block(W, BW, m0, kind, dst, tab_loaded):
            rope = kind in ("ropeA", "ropeI")
            R = self.Rattn if kind == "ropeA" else self.Ridx
            pend = {}

            def proj(tb):
                pbi = self.cnt_e % 3
                self.cnt_e += 1
                ps = self.pb[pbi]
                for c in range(8):
                    self.mm(ps[:, :], W[:, c, m0:m0 + 128], hT[:, c, tb * 512:(tb + 1) * 512], c == 0, c == 7,
                            [BW, BhT[tb]], [self.PB[pbi]])
                tsl = slice(tb * 512, (tb + 1) * 512)
                if rope:
                    s = self.cnt_o % 2
                    so = self.cnt_o % 6
                    self.cnt_o += 1
                    self.cp("act", qb_[s], ps[:, :], [self.PB[pbi]], [Bqb[s]])
                    self.tt("dve", t1[s], ps[:, :], tabc[:, tsl], ALU.mult, [self.PB[pbi], Btab, Bqb[s]], [Bt1[s]])
                    pend[tb] = (s, so)
                elif kind == "copy":
                    so = self.cnt_o % 6
                    self.cnt_o += 1
                    self.cp("act", ob[so], ps[:, :], [self.PB[pbi]], [Bob[so]])
                    self.stor(dst[:, tsl], ob[so], Bob[so], self.dram)
                elif kind == "silu":
                    so = self.cnt_o % 6
                    self.cnt_o += 1
                    self.act(ob[so], ps[:, :], AF.Silu, [self.PB[pbi]], [Bob[so]])
                    self.stor(dst[:, tsl], ob[so], Bob[so], self.dram)
                elif kind == "sigm":
                    so = self.cnt_f % 6
                    self.cnt_f += 1
                    self.act(of[so], ps[:, :], AF.Sigmoid, [self.PB[pbi]], [Bof[so]])
                    self.stor(dst[:, tsl], of[so], Bof[so], self.dram)

            def ropef(tb):
                s, so = pend.pop(tb)
                tsl = slice(tb * 512, (tb + 1) * 512)
                pr = 3 + (so % 2)
                self.mm(self.pb[pr][:, :], R, qb_[s], True, True, [self.Bc, Bqb[s]], [self.PB[pr]])
                self.tt("dve", t2[s], self.pb[pr][:, :], tabs_[:, tsl], ALU.mult, [self.PB[pr], Btab], [Bt2[s]])
                self.tt("pool", ob[so], t1[s], t2[s], ALU.add, [Bt1[s], Bt2[s]], [Bob[so]])
                self.stor(dst[:, tsl], ob[so], Bob[so], self.dram)

            for i in range(9):
                if i < 8:
                    proj(i)
                if rope and i >= 1:
                    ropef(i - 1)

        def load_tabs(ti):
            self.ld(tabc, S["tabs"][2 * ti], Btab)
            self.ld(tabs_, S["tabs"][2 * ti + 1], Btab)

        if self.stop == "B1":
            load_tabs(0)
            W, BW = load_w(O_Q, 256)
            dtab = self.nc.dram_tensor("dbg_tab", [2, 128, T], F32, kind="ExternalOutput").ap()
            bb = Buf("dbgtab")
            self.P.dma("sp", lambda e: e.dma_start(out=dtab[0], in_=tabc), [Btab], [bb], bb)
            self.P.dma("sp", lambda e: e.dma_start(out=dtab[1], in_=tabs_), [Btab], [bb], bb)
            fm_block(W, BW, 0, "copy", S["kT"][0], True)
            fm_block(W, BW, 0, "ropeA", S["qT"][0], True)
            W, BW = load_w(O_IW, 16)
            for tt_ in range(2):
                pbi = self.cnt_e % 3
                self.cnt_e += 1
                ps = self.pb[pbi]
                for c in range(8):
                    self.mm(ps[:, 0:16], hT[:, c, tt_ * 128:(tt_ + 1) * 128], W[:, c, 0:16], c == 0, c == 7,
                            [BW, BhT[tt_ // 4]], [self.PB[pbi]])
                so = self.cnt_f % 6
                self.cnt_f += 1
                self.cp("act", of[so][:, 0:16], ps[:, 0:16], [self.PB[pbi]], [Bof[so]])
                self.stor(S["iw"][tt_ * 128:(tt_ + 1) * 128, :], of[so][:, 0:16], Bof[so], self.dram)
            self.base_after_hT = base_after_hT
            return
        load_tabs(0)
        for grp, off, dst in (("q", O_Q, S["qT"]), ("k", O_K, S["kT"])):
            for cbk in range(4):
                W, BW = load_w(off + cbk * 256, 256)
                for m in range(2):
                    fm_block(W, BW, m * 128, "ropeA", dst[cbk * 2 + m], True)
        load_tabs(1)
        for cbk in range(4):
            W, BW = load_w(O_IQ + cbk * 256, 256)
            for m in range(2):
                fm_block(W, BW, m * 128, "ropeI", S["iqT"][cbk * 2 + m], True)
        W, BW = load_w(O_IK, 64, dup=True)
        fm_block(W, BW, 0, "ropeI", S["ikT"], True)
        for cbk in range(4):
            W, BW = load_w(O_GA + cbk * 256, 256)
            for m in range(2):
                fm_block(W, BW, m * 128, "silu", S["sgaT"][cbk * 2 + m], False)
        for off, dst in ((O_MA, S["smaT"]), (O_MB, S["smbT"])):
            for cbk in range(4):
                W, BW = load_w(off + cbk * 256, 256)
                for m in range(2):
                    fm_block(W, BW, m * 128, "sigm", dst[cbk * 2 + m], False)
        for cbk in range(4):
            W, BW = load_w(O_V + cbk * 256, 256)
            for tt_ in range(NT):
                pbi = self.cnt_e % 3
                self.cnt_e += 1
                ps = self.pb[pbi]
                for c in range(8):
                    self.mm(ps[:, 0:256], hT[:, c, tt_ * 128:(tt_ + 1) * 128], W[:, c, 0:256], c == 0, c == 7,
                            [BW, BhT[tt_ // 4]], [self.PB[pbi]])
                so = self.cnt_o % 6
                self.cnt_o += 1
                self.cp("act", ob[so][:, 0:256], ps[:, 0:256], [self.PB[pbi]], [Bob[so]])
                self.stor(S["v"][tt_ * 128:(tt_ + 1) * 128, cbk * 256:(cbk + 1) * 256], ob[so][:, 0:256], Bob[so], self.dram)
        W, BW = load_w(O_IW, 16)
        for tt_ in range(NT):
            pbi = self.cnt_e % 3
            self.cnt_e += 1
            ps = self.pb[pbi]
            for c in range(8):
                self.mm(ps[:, 0:16], hT[:, c, tt_ * 128:(tt_ + 1) * 128], W[:, c, 0:16], c == 0, c == 7,
                        [BW, BhT[tt_ // 4]], [self.PB[pbi]])
            so = self.cnt_f % 6
            self.cnt_f += 1
            self.cp("act", of[so][:, 0:16], ps[:, 0:16], [self.PB[pbi]], [Bof[so]])
            self.stor(S["iw"][tt_ * 128:(tt_ + 1) * 128, :], of[so][:, 0:16], Bof[so], self.dram)
        self.base_after_hT = base_after_hT

    def phaseC(self, l):
        P, I, S = self.P, self.I, self.S
        hT, BhT = self.hT, self.BhT
        P.barrier()
        self.arena_reset(self.base_after_hT)
        w_in = I["w_in"]
        cw = self.al(NB * 4, parts=BD).rearrange("p (n k) -> p n k", k=4)
        chv = self.al(4 * NB, parts=BD).rearrange("p (v n) -> p v n", v=4)
        Bcw = Buf("cw")
        Bchv = Buf("chv")
        self.ld(cw, I["convw"][l].rearrange("p (n k) -> p n k", k=4), Bcw)
        self.ld(chv, I["chv"][l].rearrange("p (v n) -> p v n", v=4), Bchv)
        clam = self.al(2 * NB, parts=BD).rearrange("p (v n) -> p v n", v=2)
        Bcl = Buf("clam")
        tmpc = self.al(NB, parts=BD)
        Btc = Buf("tmpc")
        self.act(tmpc, chv[:, 3, :], AF.Exp, [Bchv], [Btc], scale=-1.0)
        self.act(tmpc, tmpc, AF.Ln, [Btc], [Btc], bias=1.0)
        self.ts("dve", clam[:, 0, :], tmpc, -8.0, None, ALU.mult, None, [Btc], [Bcl])
        self.ts("dve", clam[:, 1, :], tmpc, -16.0, None, ALU.mult, None, [Btc], [Bcl])
        gst = self.al(NB * BD, parts=BD).rearrange("p (n d) -> p n d", n=NB)
        Bgst = Buf("gst")
        wr = self.alb(NB * BD, parts=BD).rearrange("p (n d) -> p n d", n=NB)
        wi = self.alb(NB * BD, parts=BD).rearrange("p (n d) -> p n d", n=NB)
        Bwr, Bwi = Buf("wr"), Buf("wi")
        self.ld(gst, I["w_rg"][l].rearrange("n c d -> c n d"), Bgst)
        self.cp("act", wr, gst, [Bgst], [Bwr])
        self.ld(gst, I["w_ig"][l].rearrange("n c d -> c n d"), Bgst)
        self.cp("act", wi, gst, [Bgst], [Bwi])
        wst = [self.al(8 * 2 * BD).rearrange("p (c n) -> p c n", c=8) for _ in range(2)]
        Bwst = [Buf("cwst%d" % i) for i in range(2)]
        wxg = [self.alb(8 * 2 * BD).rearrange("p (c n) -> p c n", c=8) for _ in range(2)]
        Bwxg = [Buf("wxg%d" % i) for i in range(2)]
        xrb = [self.al(515, parts=BD) for _ in range(3)]
        Bxrb = [Buf("xrb%d" % i) for i in range(3)]
        xc = [self.al(512, parts=BD) for _ in range(3)]
        Bxc = [Buf("xc%d" % i) for i in range(3)]
        xcb = [self.alb(512, parts=BD) for _ in range(2)]
        Bxcb = [Buf("xcb%d" % i) for i in range(2)]
        rr = [self.al(512, parts=BD) for _ in range(2)]
        Brr = [Buf("rr%d" % i) for i in range(2)]
        ii = [self.al(512, parts=BD) for _ in range(2)]
        Bii = [Buf("ii%d" % i) for i in range(2)]
        aa = [self.al(512, parts=BD) for _ in range(2)]
        Baa = [Buf("aa%d" % i) for i in range(2)]
        sq = [self.al(512, parts=BD) for _ in range(2)]
        Bsq = [Buf("sq%d" % i) for i in range(2)]
        uu = [self.al(512, parts=BD) for _ in range(2)]
        Buu = [Buf("uu%d" % i) for i in range(2)]
        hh = [self.al(512, parts=BD) for _ in range(2)]
        Bhh = [Buf("hh%d" % i) for i in range(2)]
        sg = [self.al(512, parts=BD) for _ in range(2)]
        Bsg = [Buf("sg%d" % i) for i in range(2)]
        yb = [self.alb(512, parts=BD) for _ in range(3)]
        Byb = [Buf("yb%d" % i) for i in range(3)]
        hb = self.al(3 * NB, parts=BD).rearrange("p (v n) -> p v n", v=3)
        Bhb = Buf("hb")
        self.ts("dve", hb[:, 0:2, :], chv[:, 1:3, :], 0.5, None, ALU.mult, None, [Bchv], [Bhb])
        self.ts("dve", hb[:, 2, :], clam[:, 0, :], 0.5, None, ALU.mult, None, [Bcl], [Bhb])
        ctmp = self.al(512, parts=BD)
        Bctmp = Buf("ctmp")
        nbk = self.lim.get('C_nb', NB)
        iters = [(n, tb) for n in range(nbk) for tb in range(8)]

        def Sa_pe(it):
            n, tb = iters[it]
            ws = n % 2
            s3 = it % 3
            pA = it % 2
            tsl = slice(tb * 512, (tb + 1) * 512)
            def loadw(nn):
                w2 = nn % 2
                self.ld(wst[w2][:, :, 0:BD], w_in[l, :, O_XR + nn * BD:O_XR + (nn + 1) * BD].rearrange("(c p) n -> p c n", p=128), Bwst[w2])
                self.ld(wst[w2][:, :, BD:2 * BD], w_in[l, :, O_GR + nn * BD:O_GR + (nn + 1) * BD].rearrange("(c p) n -> p c n", p=128), Bwst[w2])
                self.cp("act", wxg[w2], wst[w2], [Bwst[w2]], [Bwxg[w2]])
            if it == 0:
                loadw(0)
            if tb == 4 and n + 1 < nbk:
                loadw(n + 1)
            for c in range(8):
                self.mm(self.pb[pA][0:BD, :], wxg[ws][:, c, 0:BD], hT[:, c, tsl], c == 0, c == 7,
                        [Bwxg[ws], BhT[tb]], [self.PB[pA]])

        def Sa_rest(it):
            n, tb = iters[it]
            s3 = it % 3
            pA = it % 2
            if tb == 0:
                self.P.op("pool", (lambda o: (lambda e: e.memset(o, 0.0)))(xrb[s3][:, 0:3]), [], [Bxrb[s3]])
            else:
                sp = (it - 1) % 3
                self.cp("pool", xrb[s3][:, 0:3], xrb[sp][:, 512:515], [Bxrb[sp]], [Bxrb[s3]])
            self.cp("act", xrb[s3][:, 3:515], self.pb[pA][0:BD, :], [self.PB[pA]], [Bxrb[s3]])
            self.ts("pool", xc[s3], xrb[s3][:, 3:515], cw[:, n, 3:4], chv[:, 0, n:n + 1], ALU.mult, ALU.add,
                    [Bxrb[s3], Bcw, Bchv], [Bxc[s3]])
            self.ts("pool", ctmp, xrb[s3][:, 2:514], cw[:, n, 2:3], 0.0, ALU.mult, ALU.add, [Bxrb[s3], Bcw], [Bctmp])
            self.tt("pool", xc[s3], xc[s3], ctmp, ALU.add, [Bxc[s3], Bctmp], [Bxc[s3]])

        def Sb(it):
            n, tb = iters[it]
            ws = n % 2
            s3 = it % 3
            s = it % 2
            pB, pC, pD = 2 + s, 4 + s, 6 + s
            tsl = slice(tb * 512, (tb + 1) * 512)
            for k in (1, 0):
                self.stt(xc[s3], xrb[s3][:, k:k + 512], cw[:, n, k:k + 1], xc[s3], ALU.mult, ALU.add,
                         [Bxrb[s3], Bcw, Bxc[s3]], [Bxc[s3]])
            self.cp("pool", xcb[s], xc[s3], [Bxc[s3]], [Bxcb[s]])
            for c in range(8):
                self.mm(self.pb[pD][0:BD, :], wxg[ws][:, c, BD:2 * BD], hT[:, c, tsl], c == 0, c == 7,
                        [Bwxg[ws], BhT[tb]], [self.PB[pD]])

        def Sg(it):
            n, tb = iters[it]
            s = it % 2
            pB, pC = 2 + s, 4 + s
            self.mm(self.pb[pB][0:BD, :], wr[:, n, :], xcb[s], True, True, [Bwr, Bxcb[s]], [self.PB[pB]])
            self.mm(self.pb[pC][0:BD, :], wi[:, n, :], xcb[s], True, True, [Bwi, Bxcb[s]], [self.PB[pC]])

        def Sc(it):
            n, tb = iters[it]
            s3 = it % 3
            s = it % 2
            pB, pC, pD = 2 + s, 4 + s, 6 + s
            self.act(rr[s], self.pb[pB][0:BD, :], AF.Tanh, [self.PB[pB], Bhb], [Brr[s]], scale=0.5, bias=hb[:, 0, n:n + 1])
            self.act(ii[s], self.pb[pC][0:BD, :], AF.Tanh, [self.PB[pC], Bhb], [Bii[s]], scale=0.5, bias=hb[:, 1, n:n + 1])
            self.act(sg[s], self.pb[pD][0:BD, :], AF.Tanh, [self.PB[pD]], [Bsg[s]], scale=0.5)
            self.act(aa[s], rr[s], AF.Exp, [Brr[s], Bhb], [Baa[s]], scale=hb[:, 2, n:n + 1], bias=hb[:, 2, n:n + 1])
            self.act(sq[s], rr[s], AF.Exp, [Brr[s], Bcl], [Bsq[s]], scale=clam[:, 0, n:n + 1], bias=clam[:, 0, n:n + 1])
            self.act(sq[s], sq[s], AF.Ln, [Bsq[s]], [Bsq[s]], scale=-1.0, bias=1.0)
            self.act(sq[s], sq[s], AF.Exp, [Bsq[s]], [Bsq[s]], scale=0.5)
            self.stt(uu[s], ii[s], 1.0, xc[s3], ALU.add, ALU.mult, [Bii[s], Bxc[s3]], [Buu[s]])
            self.stt(sg[s], sg[s], 1.0, self.pb[pD][0:BD, :], ALU.add, ALU.mult, [Bsg[s], self.PB[pD]], [Bsg[s]])

        def Sd(it):
            n, tb = iters[it]
            s = it % 2
            sy = it % 3
            tsl = slice(tb * 512, (tb + 1) * 512)
            self.stt(uu[s], uu[s], 0.5, sq[s], ALU.mult, ALU.mult, [Buu[s], Bsq[s]], [Buu[s]])
            init = 0.0 if tb == 0 else hh[1 - s][:, 511:512]
            rd = [Baa[s], Buu[s]] + ([] if tb == 0 else [Bhh[1 - s]])
            self.P.op("dve", (lambda o, a_, u_, i_: (lambda e: e.tensor_tensor_scan(out=o, data0=a_, data1=u_, initial=i_, op0=ALU.mult, op1=ALU.add)))(hh[s], aa[s], uu[s], init), rd, [Bhh[s]])
            self.stt(yb[sy], hh[s], 0.5, sg[s], ALU.mult, ALU.mult, [Bhh[s], Bsg[s]], [Byb[sy]])
            self.stor(S["ybT"][n][:, tsl], yb[sy], Byb[sy], self.dram, eng="sp")

        NI = len(iters)
        for t in range(-3, NI + 1):
            if 0 <= t < NI:
                Sg(t)
            if 0 <= t + 2 < NI:
                Sa_rest(t + 2)
            if 0 <= t + 1 < NI:
                Sb(t + 1)
            if 0 <= t < NI:
                Sc(t)
            if 0 <= t - 1 < NI:
                Sd(t - 1)
            if 0 <= t + 3 < NI:
                Sa_pe(t + 3)

    def phaseD1(self, l):
        P, I, S = self.P, self.I, self.S
        P.new_epoch()
        self.arena_reset()
        ikT = self.alb(T)
        Bik = Buf("ikT")
        self.ld(ikT, S["ikT"][:, :], Bik)
        iqb = [self.alb(8 * 512).rearrange("p (c t) -> p c t", c=8) for _ in range(2)]
        Biq = [Buf("iqb%d" % i) for i in range(2)]
        iwb = [self.al(64).rearrange("p (t h) -> p t h", t=4) for _ in range(2)]
        Biw = [Buf("iwb%d" % i) for i in range(2)]
        dg = [self.alb(16 * 128).rearrange("p (h q) -> p h q", h=16) for _ in range(2)]
        Bdg = [Buf("dg%d" % i) for i in range(2)]
        rl = [self.alb(512) for _ in range(4)]
        Brl = [Buf("rl%d" % i) for i in range(4)]
        sc = [self.al(T) for _ in range(2)]
        Bsc = [Buf("sc%d" % i) for i in range(2)]
        junk = self.alb(T)
        Bj = Buf("junkb")
        mrow = [self.alb(T) for _ in range(2)]
        Bmr = [Buf("mrow%d" % i) for i in range(2)]
        mT = [self.alb(32 * 512).rearrange("p (k q) -> p k q", k=32) for _ in range(2)]
        BmT = [Buf("mT%d" % i) for i in range(2)]
        sm = [self.al(16 + NIT + 1) for _ in range(2)]
        Bsm = [Buf("sm%d" % i) for i in range(2)]
        st_ = {"nrl": 0, "npd": 0}
        nqb = self.lim.get('D1_qb', 8)

        def S1(j):
            qb, jj = divmod(j, 4)
            qs = qb % 2
            js = j % 2
            nk = (j + 1) * 128
            if jj == 0:
                self.ld(iqb[qs], S["iqT"][:, :, qb * 512:(qb + 1) * 512].rearrange("c p t -> p c t"), Biq[qs])
                self.ld(iwb[qs], S["iw"][qb * 512:(qb + 1) * 512, :].rearrange("(t p) h -> p t h", p=128), Biw[qs])
            for h in range(16):
                self.act(dg[js][:, h, :], self.ident, AF.Copy, [self.Bc, Biw[qs]], [Bdg[js]], scale=iwb[qs][:, jj, h:h + 1])
            nch = (nk + 511) // 512
            items = [(c, h) for c in range(nch) for h in range(16)]
            slots = {}

            def dots(c, h):
                w = min(512, nk - c * 512)
                pd = st_["npd"] % 4
                st_["npd"] += 1
                r0 = (h % 2) * 64
                self.mm(self.pb[pd][:, 0:w], iqb[qs][r0:r0 + 64, h // 2, jj * 128:(jj + 1) * 128],
                        ikT[r0:r0 + 64, c * 512:c * 512 + w], True, True, [Biq[qs], Bik], [self.PB[pd]])
                rs = st_["nrl"] % 4
                st_["nrl"] += 1
                self.act(rl[rs][:, 0:w], self.pb[pd][:, 0:w], AF.Relu, [self.PB[pd]], [Brl[rs]])
                slots[(c, h)] = rs

            def acc(c, h):
                w = min(512, nk - c * 512)
                pacc = 6 + (c % 2)
                rs = slots.pop((c, h))
                self.mm(self.pb[pacc][:, 0:w], dg[js][:, h, :], rl[rs][:, 0:w], h == 0, h == 15,
                        [Bdg[js], Brl[rs]], [self.PB[pacc]])
                if h == 15:
                    self.cp("act", sc[js][:, c * 512:c * 512 + w], self.pb[pacc][:, 0:w], [self.PB[pacc]], [Bsc[js]])

            npair = len(items) // 2
            for pi in range(npair + 1):
                if pi < npair:
                    dots(*items[2 * pi])
                    dots(*items[2 * pi + 1])
                if pi >= 1:
                    acc(*items[2 * pi - 2])
                    acc(*items[2 * pi - 1])

        def S2(j):
            js = j % 2
            nk = (j + 1) * 128
            s_ = sm[js]
            B_ = Bsm[js]
            diag = sc[js][:, j * 128:(j + 1) * 128]
            if j >= 2:
                self.P.op("dve", (lambda o, i_: (lambda e: e.tensor_reduce(out=o, in_=i_, axis=AX.X, op=ALU.max)))(s_[:, 0:1], sc[js][:, 0:nk]), [Bsc[js]], [B_])
                self.P.op("dve", (lambda o, i_: (lambda e: e.tensor_reduce(out=o, in_=i_, axis=AX.X, op=ALU.min)))(s_[:, 1:2], sc[js][:, 0:nk]), [Bsc[js]], [B_])
                self.tt("dve", diag, diag, self.causal, ALU.add, [Bsc[js], self.Bc2], [Bsc[js]])
                self.tt("dve", s_[:, 2:3], s_[:, 0:1], s_[:, 1:2], ALU.subtract, [B_], [B_])
                steps = s_[:, 16:16 + NIT + 1]
                self.ts("dve", steps, self.pow2, s_[:, 2:3], None, ALU.mult, None, [self.Bc2, B_], [B_])
                self.tt("dve", s_[:, 3:4], s_[:, 1:2], steps[:, 0:1], ALU.add, [B_], [B_])
                for k in range(NIT):
                    self.ts("dve", junk[:, 0:nk], sc[js][:, 0:nk], s_[:, 3:4], 0.0, ALU.is_ge, ALU.add,
                            [Bsc[js], B_], [Bj, B_], accum_out=s_[:, 4:5])
                    self.ts("dve", s_[:, 5:6], s_[:, 4:5], float(TOPK) - 0.5, -0.5, ALU.is_ge, ALU.add, [B_], [B_])
                    self.stt(s_[:, 3:4], s_[:, 5:6], steps[:, k:k + 1], s_[:, 3:4], ALU.mult, ALU.add, [B_], [B_])
                self.tt("dve", s_[:, 6:7], s_[:, 3:4], steps[:, NIT:NIT + 1], ALU.subtract, [B_], [B_])
            else:
                self.tt("dve", diag, diag, self.causal, ALU.add, [Bsc[js], self.Bc2], [Bsc[js]])
                self.P.op("dve", (lambda o: (lambda e: e.memset(o, -1e29)))(s_[:, 6:7]), [], [B_])
            self.ts("pool", mrow[js][:, 0:nk], sc[js][:, 0:nk], s_[:, 6:7], 1.0, ALU.is_ge, ALU.mult,
                    [Bsc[js], B_], [Bmr[js]])

        def S3(j):
            qb, jj = divmod(j, 4)
            qs = qb % 2
            js = j % 2
            for k0 in range(0, j + 1, 4):
                kn = min(4, j + 1 - k0)
                pt = 4 + ((k0 // 4) % 2)
                pv = self.pb[pt][:].bitcast(BF)
                for kk in range(kn):
                    kt = k0 + kk
                    self.tr(pv[:, kk * 128:(kk + 1) * 128], mrow[js][:, kt * 128:(kt + 1) * 128], self.ident,
                            [Bmr[js], self.Bc], [self.PB[pt]])
                self.cp("act", mT[qs][:, k0:k0 + kn, jj * 128:(jj + 1) * 128],
                        pv[:, 0:kn * 128].rearrange("p (k q) -> p k q", k=kn), [self.PB[pt]], [(BmT[qs], True)])
            if jj == 3:
                nkt = 4 * qb + 4
                self.stor(S["maskT"][qb][:, 0:nkt * 512].rearrange("p (k q) -> p k q", k=nkt), mT[qs][:, 0:nkt, :], BmT[qs], self.dram, eng="sp")

        for j in range(nqb * 4):
            S1(j)
            if j > 0:
                S3(j - 1)
            S2(j)
        S3(nqb * 4 - 1)

    def phaseD2(self, l):
        P, I, S = self.P, self.I, self.S
        P.barrier()
        self.arena_reset()
        kT = [self.alb(T) for _ in range(2)]
        BkT = [Buf("kT%d" % i) for i in range(2)]
        vv = [self.alb(32 * 128).rearrange("p (t d) -> p t d", t=32) for _ in range(2)]
        Bvv = [Buf("vv%d" % i) for i in range(2)]
        mT = [self.alb(32 * 512).rearrange("p (k q) -> p k q", k=32) for _ in range(2)]
        BmT = [Buf("mT%d" % i) for i in range(2)]
        qT = [self.alb(512) for _ in range(2)]
        BqT = [Buf("qT%d" % i) for i in range(2)]
        sga = [self.alb(512) for _ in range(2)]
        Bsga = [Buf("sga%d" % i) for i in range(2)]
        pt = [self.alb(512) for _ in range(4)]
        Bpt = [Buf("pt%d" % i) for i in range(4)]
        lnb = [self.al(512) for _ in range(2)]
        Bln = [Buf("ln%d" % i) for i in range(2)]
        ot = [self.al(512) for _ in range(2)]
        Bot = [Buf("ot%d" % i) for i in range(2)]
        go = [self.alb(512) for _ in range(3)]
        Bgo = [Buf("go%d" % i) for i in range(3)]
        scale = float(128 ** -0.5)
        it = 0
        st2 = {"npt": 0, "nS": 0}
        for h in range(self.lim.get('D2_h', 8)):
            hs = h % 2
            self.ld(kT[hs], S["kT"][h], BkT[hs])
            for q4 in range(4):
                self.P.dma("sp", (lambda o, i_: (lambda e: e.dma_start(out=o, in_=i_)))(
                    vv[hs][:, q4 * 8:(q4 + 1) * 8, :],
                    S["v"][q4 * 1024:(q4 + 1) * 1024, h * 128:(h + 1) * 128].rearrange("(t p) d -> p t d", p=128)),
                    [], [(Bvv[hs], q4 > 0)], Bvv[hs])
            for qb in range(self.lim.get('D2_qb', 8)):
                s = it % 2
                s3 = it % 3
                it += 1
                nkt = 4 * qb + 4
                qsl = slice(qb * 512, (qb + 1) * 512)
                self.ld(mT[s][:, 0:nkt, :], S["maskT"][qb][:, 0:nkt * 512].rearrange("p (k q) -> p k q", k=nkt), BmT[s])
                self.ld(qT[s], S["qT"][h][:, qsl], BqT[s])
                self.ld(sga[s], S["sgaT"][h][:, qsl], Bsga[s])
                pO = 4 + s * 2
                pS_ = 5 + s * 2
                slots = {}

                def qk(kt, s=s, qb=qb, hs=hs):
                    i0 = max(0, kt - 4 * qb)
                    cs = slice(i0 * 128, 512)
                    pS = st2["nS"] % 4
                    st2["nS"] += 1
                    self.mm(self.pb[pS][:, cs], kT[hs][:, kt * 128:(kt + 1) * 128], qT[s][:, cs], True, True,
                            [BkT[hs], BqT[s]], [self.PB[pS]])
                    ps_ = st2["npt"] % 4
                    st2["npt"] += 1
                    self.act(pt[ps_][:, cs], self.pb[pS][:, cs], AF.Exp, [self.PB[pS]], [Bpt[ps_]], scale=scale)
                    self.tt("dve", pt[ps_][:, cs], pt[ps_][:, cs], mT[s][:, kt, cs], ALU.mult, [Bpt[ps_], BmT[s]], [Bpt[ps_]])
                    slots[kt] = (ps_, cs)

                def pv(kt, s=s, hs=hs, nkt=nkt, pO=pO, pS_=pS_):
                    ps_, cs = slots.pop(kt)
                    self.mm(self.pb[pO][:, cs], vv[hs][:, kt, :], pt[ps_][:, cs], kt == 0, kt == nkt - 1,
                            [Bvv[hs], Bpt[ps_]], [self.PB[pO]])
                    self.mm(self.pb[pS_][:, cs], self.ones, pt[ps_][:, cs], kt == 0, kt == nkt - 1,
                            [self.Bc, Bpt[ps_]], [self.PB[pS_]])

                LA = 2
                for i in range(nkt + LA):
                    if i < nkt:
                        qk(i)
                    if i >= LA:
                        pv(i - LA)
                self.act(lnb[s], self.pb[pS_][:, :], AF.Ln, [self.PB[pS_]], [Bln[s]])
                self.act(lnb[s], lnb[s], AF.Exp, [Bln[s]], [Bln[s]], scale=-1.0)
                self.tt("dve", ot[s], self.pb[pO][:, :], lnb[s], ALU.mult, [self.PB[pO], Bln[s]], [Bot[s]])
                self.tt("dve", go[s3], ot[s], sga[s], ALU.mult, [Bot[s], Bsga[s]], [Bgo[s3]])
                self.stor(S["gaoT"][h][:, qsl], go[s3], Bgo[s3], self.dram, eng="pool")

    def phaseE(self, l, xin, xout, last):
        P, I, S = self.P, self.I, self.S
        P.barrier()
        self.arena_reset()
        woa = self.alb(8 * D).rearrange("p (c n) -> p c n", c=8)
        NC_R = DR // 128
        wor = self.alb(NC_R * D).rearrange("p (c n) -> p c n", c=NC_R)
        wo = self.alb(8 * D).rearrange("p (c n) -> p c n", c=8)
        Bwoa, Bwor, Bwo = Buf("woa"), Buf("wor"), Buf("wo")
        wst = [self.al(D) for _ in range(2)]
        Bwst = [Buf("ewst%d" % i) for i in range(2)]
        nw = 0
        for c in range(8):
            s = nw % 2
            nw += 1
            self.ld(wst[s], I["w_oa"][l, c * 128:(c + 1) * 128, :], Bwst[s])
            self.cp("act", woa[:, c, :], wst[s], [Bwst[s]], [(Bwoa, True)])
        for c in range(NC_R):
            s = nw % 2
            nw += 1
            self.ld(wst[s], I["w_or"][l, c * 128:(c + 1) * 128, :], Bwst[s])
            self.cp("act", wor[:, c, :], wst[s], [Bwst[s]], [(Bwor, True)])
        for c in range(8):
            s = nw % 2
            nw += 1
            self.ld(wst[s], I["w_o"][l, c * 128:(c + 1) * 128, :], Bwst[s])
            self.cp("act", wo[:, c, :], wst[s], [Bwst[s]], [(Bwo, True)])
        fg = None
        if last:
            fg = self.al(D)
            Bfg = Buf("fg")
            self.ld(fg, I["final_g"][0:1, :].broadcast_to([128, D]), Bfg)
        gao = [self.alb(8 * 512).rearrange("p (c t) -> p c t", c=8) for _ in range(2)]
        Bgao = [Buf("gao%d" % i) for i in range(2)]
        ybb = [self.alb(NC_R * 512).rearrange("p (c t) -> p c t", c=NC_R) for _ in range(2)]
        Bybb = [Buf("ybb%d" % i) for i in range(2)]
        sma = [self.al(512) for _ in range(2)]
        Bsma = [Buf("sma%d" % i) for i in range(2)]
        smb = [self.al(512) for _ in range(2)]
        Bsmb = [Buf("smb%d" % i) for i in range(2)]
        t1 = [self.al(512) for _ in range(2)]
        Bt1 = [Buf("et1%d" % i) for i in range(2)]
        t2 = [self.al(512) for _ in range(2)]
        Bt2 = [Buf("et2%d" % i) for i in range(2)]
        mT = [self.alb(8 * 512).rearrange("p (c t) -> p c t", c=8) for _ in range(2)]
        BmT = [Buf("emT%d" % i) for i in range(2)]
        xs = [self.al(D) for _ in range(3)]
        Bxs = [Buf("exs%d" % i) for i in range(3)]
        xo = [self.al(D) for _ in range(3)]
        Bxo = [Buf("exo%d" % i) for i in range(3)]
        junk = self.al(D)
        Bj = Buf("ejunk")
        sm = self.al(8)
        io = 0
        ix = 0
        for tb in range(self.lim.get('E_tb', 8)):
            s = tb % 2
            tsl = slice(tb * 512, (tb + 1) * 512)
            self.ld(gao[s], S["gaoT"][:, :, tsl].rearrange("c p t -> p c t"), Bgao[s])
            self.ld(ybb[s], S["ybT"].rearrange("n c t -> (n c) t")[:, tsl].rearrange("(c p) t -> p c t", p=128), Bybb[s])
            for oc in range(8):
                so = io % 2
                io += 1
                self.ld(sma[so], S["smaT"][oc][:, tsl], Bsma[so])
                self.ld(smb[so], S["smbT"][oc][:, tsl], Bsmb[so])
                pA = (io % 2) * 2
                pB = pA + 1
                for c in range(8):
                    self.mm(self.pb[pA][:, :], woa[:, c, oc * 128:(oc + 1) * 128], gao[s][:, c, :], c == 0, c == 7,
                            [Bwoa, Bgao[s]], [self.PB[pA]])
                for c in range(NC_R):
                    self.mm(self.pb[pB][:, :], wor[:, c, oc * 128:(oc + 1) * 128], ybb[s][:, c, :], c == 0, c == NC_R - 1,
                            [Bwor, Bybb[s]], [self.PB[pB]])
                self.tt("dve", t1[so], self.pb[pA][:, :], sma[so], ALU.mult, [self.PB[pA], Bsma[so]], [Bt1[so]])
                self.tt("dve", t2[so], self.pb[pB][:, :], smb[so], ALU.mult, [self.PB[pB], Bsmb[so]], [Bt2[so]])
                self.tt("dve", mT[s][:, oc, :], t1[so], t2[so], ALU.add, [Bt1[so], Bt2[so]], [(BmT[s], True)])
            for tq in range(4):
                sx = ix % 3
                ix += 1
                tt_ = tb * 4 + tq
                self.ld(xs[sx], xin[tt_ * 128:(tt_ + 1) * 128, :], Bxs[sx])
                for half in range(2):
                    pO = 4 + (ix % 2) * 2 + half
                    for c in range(8):
                        self.mm(self.pb[pO][:, :], mT[s][:, c, tq * 128:(tq + 1) * 128], wo[:, c, half * 512:(half + 1) * 512],
                                c == 0, c == 7, [BmT[s], Bwo], [self.PB[pO]])
                    self.tt("dve", xo[sx][:, half * 512:(half + 1) * 512], self.pb[pO][:, :], xs[sx][:, half * 512:(half + 1) * 512],
                            ALU.add, [self.PB[pO], Bxs[sx]], [Bxo[sx]])
                if last:
                    Bs_ = Buf("ess")
                    ss = sm[:, 0:1]
                    self.act(junk, xo[sx], AF.Square, [Bxo[sx]], [Bj, Bs_], accum_out=ss)
                    self.ts("dve", sm[:, 1:2], ss, 1.0 / D, EPS, ALU.mult, ALU.add, [Bs_], [Bs_])
                    self.act(sm[:, 2:3], sm[:, 1:2], AF.Sqrt, [Bs_], [Bs_])
                    self.P.op("dve", (lambda o, i_: (lambda e: e.reciprocal(out=o, in_=i_)))(sm[:, 3:4], sm[:, 2:3]), [Bs_], [Bs_])
                    self.stt(xo[sx], xo[sx], sm[:, 3:4], fg, ALU.mult, ALU.mult, [Bxo[sx], Bs_, Bfg], [Bxo[sx]])
                self.stor(xout[tt_ * 128:(tt_ + 1) * 128, :], xo[sx], Bxo[sx], self.dram, eng="pool")


def _consts():
    ident = np.eye(128, dtype=np.float32)
    Ra = np.zeros((128, 128), np.float32)
    for dp in range(64):
        Ra[dp + 64, dp] = -1.0
        Ra[dp, dp + 64] = 1.0
    Ri = np.zeros((128, 128), np.float32)
    for hh in range(2):
        for dp in range(32):
            Ri[hh * 64 + dp + 32, hh * 64 + dp] = -1.0
            Ri[hh * 64 + dp, hh * 64 + dp + 32] = 1.0
    ones = np.ones((128, 128), np.float32)
    cbf = np.concatenate([ident, Ra, Ri, ones], axis=1).astype(ml_dtypes.bfloat16)
    causal = np.where(np.arange(128)[None, :] <= np.arange(128)[:, None], 0.0, -1e30).astype(np.float32)
    p = np.arange(128)
    invA = (10000.0 ** (-(np.arange(0, 128, 2, dtype=np.float32)) / 128.0)).astype(np.float32)[p % 64]
    invI = (10000.0 ** (-(np.arange(0, 64, 2, dtype=np.float32)) / 64.0)).astype(np.float32)[(p % 64) % 32]
    pow2 = np.tile((2.0 ** -(np.arange(NIT + 1) + 1.0)).astype(np.float32)[None, :], (128, 1))
    cf = np.concatenate([causal, invA[:, None], invI[:, None], pow2, np.zeros((128, 1), np.float32)], axis=1).astype(np.float32)
    return cbf, cf


def make_in_maps(x, positions, norm_g, w_in, conv_w, conv_b, w_rg, b_rg, w_ig, b_ig,
                 lru_lambda, w_out_attn, w_out_rnn, w_o, final_g, n_cores=8):
    cbf, cf = _consts()
    c = np.ascontiguousarray
    f = lambda a: c(np.asarray(a, dtype=np.float32))
    shared = {
        "norm_g": c(f(norm_g).reshape(DEPTH, 8, 128).transpose(0, 2, 1)),
        "w_in": f(w_in),
        "convw": c(f(conv_w).reshape(DEPTH, 4, NB, BD).transpose(0, 3, 2, 1).reshape(DEPTH, BD, NB * 4)),
        "chv": c(np.stack([f(conv_b), f(b_rg), f(b_ig), f(lru_lambda)], axis=1).reshape(DEPTH, 4, NB, BD)
                 .transpose(0, 3, 1, 2).reshape(DEPTH, BD, 4 * NB)),
        "w_rg": f(w_rg), "w_ig": f(w_ig),
        "w_oa": f(w_out_attn), "w_or": f(w_out_rnn), "w_o": f(w_o),
        "final_g": f(final_g).reshape(1, D),
        "cbf": cbf, "cf": cf,
    }
    maps = []
    for ci in range(n_cores):
        b = ci % 4
        m = dict(shared)
        m["x"] = f(x[b])
        m["pos"] = c(np.asarray(positions[b], dtype=np.int32).reshape(1, T))
        maps.append(m)
    return maps


_NC_CACHE = {}


def kernel(**inputs):
    if "nc" not in _NC_CACHE:
        _NC_CACHE["nc"] = K().build()
    nc = _NC_CACHE["nc"]
    maps = make_in_maps(**inputs)
    res = run_bass_kernel_spmd(nc, maps, core_ids=list(range(8)))
    out = np.stack([np.asarray(res.results[b]["out"], dtype=np.float32) for b in range(4)], axis=0)
    return out
```

```python
import numpy as np
from contextlib import ExitStack
import ml_dtypes
import concourse.bass as bass
import concourse.mybir as mybir
from concourse.bass_utils import run_bass_kernel_spmd

F32 = mybir.dt.float32
BF = mybir.dt.bfloat16
I32 = mybir.dt.int32
AF = mybir.ActivationFunctionType
ALU = mybir.AluOpType
AX = mybir.AxisListType

T = 4096
D = 1024
NT = 32
DEPTH = 2
NIN = 10064
O_Q, O_K, O_V, O_GA, O_IQ, O_IK, O_IW, O_XR, O_GR, O_MA, O_MB = (
    0, 1024, 2048, 3072, 4096, 5120, 5184, 5200, 6608, 8016, 9040)
DR = 1408
NB = 16
BD = 88
TOPK = 256
NIT = 16
EPS = 1e-6
NEG = -30000.0
ENGS = ("pe", "act", "dve", "pool", "sp")


class Buf:
    __slots__ = ("name", "w", "r", "closed", "sem")

    def __init__(self, name="b"):
        self.name = name
        self.w = {}
        self.r = {}
        self.closed = False
        self.sem = None


class Prog:
    def __init__(self, nc):
        self.nc = nc
        self.q = {e: [] for e in ENGS}
        self.cnt = {e: 0 for e in ENGS}
        self.waited = {e: {} for e in ENGS}
        self.nsem_dma = 0
        self.key = {e: e for e in ENGS}
        self.epoch = 0
        self.free_dma = []
        self.free_dma_sw = []
        self.live_dma = []

    def new_epoch(self):
        self.barrier()
        self.epoch += 1
        for e in ENGS:
            self.key[e] = "%s_%d" % (e, self.epoch)
            self.cnt[self.key[e]] = 0

    def _collect(self, reads, writes):
        deps = {}

        def add(d):
            for k, v in d.items():
                if deps.get(k, 0) < v:
                    deps[k] = v

        for b in reads:
            add(b.w)
        for b, dj in writes:
            if dj and not b.closed:
                continue
            add(b.w)
            add(b.r)
        return deps

    def _commit(self, ev, reads, writes):
        k, v = ev
        for b, dj in writes:
            if dj and not b.closed:
                if b.w.get(k, 0) < v:
                    b.w[k] = v
            else:
                b.w = {k: v}
                b.r = {}
                b.closed = False
        for b in reads:
            if b.r.get(k, 0) < v:
                b.r[k] = v
            b.closed = True

    def _emit_waits(self, eng, deps):
        wt = self.waited[eng]
        for k, v in deps.items():
            if wt.get(k, 0) < v:
                self.q[eng].append(("wait", k, v))
                wt[k] = v

    @staticmethod
    def _nw(writes):
        return [w if isinstance(w, tuple) else (w, False) for w in writes]

    def op(self, eng, fn, reads=(), writes=()):
        writes = self._nw(writes)
        self._emit_waits(eng, self._collect(reads, writes))
        k = self.key[eng]
        self.cnt[k] += 1
        self.q[eng].append(("op", fn, k, 1))
        self._commit((k, self.cnt[k]), reads, writes)

    def dma(self, eng, fn, reads, writes, slot):
        writes = self._nw(writes)
        self._emit_waits(eng, self._collect(reads, writes))
        if slot.sem is None:
            pool_ = self.free_dma_sw if eng == "pool" else self.free_dma
            if pool_:
                slot.sem = pool_.pop()
            else:
                slot.sem = ("w%d" if eng == "pool" else "d%d") % self.nsem_dma
                self.nsem_dma += 1
                self.cnt[slot.sem] = 0
            self.live_dma.append(slot)
        self.cnt[slot.sem] += 16
        self.q[eng].append(("op", fn, slot.sem, 16))
        self._commit((slot.sem, self.cnt[slot.sem]), reads, writes)

    def barrier(self):
        for e in ENGS:
            self._emit_waits(e, dict(self.cnt))
        for b in self.live_dma:
            (self.free_dma_sw if b.sem.startswith("w") else self.free_dma).append(b.sem)
            b.sem = None
        self.live_dma = []

    def emit(self, stack):
        nc = self.nc
        sems = {k: stack.enter_context(nc.semaphore("s_" + k)) for k in self.cnt}
        block = stack.enter_context(nc.Block())
        engmap = {"pe": "tensor", "act": "scalar", "dve": "vector", "pool": "gpsimd", "sp": "sync"}

        def make(e):
            def body(engobj):
                for item in self.q[e]:
                    if item[0] == "wait":
                        engobj.wait_ge(sems[item[1]], item[2])
                    else:
                        item[1](engobj).then_inc(sems[item[2]], item[3])
            return body

        for e in ENGS:
            getattr(block, engmap[e])(make(e))


class K:
    def __init__(self, debug=None, nlayers=DEPTH, stop=None, phases=None, lim=None, ext=()):
        self.stop = stop
        self.phases = phases or ["0", "A", "B", "C", "D1", "D2", "E"]
        self.lim = lim or {}
        self.ext = ext
        self.debug = debug
        self.nlayers = nlayers
        self.nc = bass.Bass("TRN2", target_bir_lowering=False)
        self.st = ExitStack()

    def act(self, out, in_, func, r, w, **kw):
        self.P.op("act", lambda e: e.activation(out=out, in_=in_, func=func, **kw), r, w)

    def ts(self, eng, out, in0, s1, s2, op0, op1, r, w, **kw):
        if op1 is None:
            self.P.op(eng, lambda e: e.tensor_scalar(out=out, in0=in0, scalar1=s1, scalar2=None, op0=op0, **kw), r, w)
        else:
            self.P.op(eng, lambda e: e.tensor_scalar(out=out, in0=in0, scalar1=s1, scalar2=s2, op0=op0, op1=op1, **kw), r, w)

    def tt(self, eng, out, in0, in1, op, r, w):
        self.P.op(eng, lambda e: e.tensor_tensor(out=out, in0=in0, in1=in1, op=op), r, w)

    def stt(self, out, in0, scalar, in1, op0, op1, r, w):
        self.P.op("dve", lambda e: e.scalar_tensor_tensor(out=out, in0=in0, scalar=scalar, in1=in1, op0=op0, op1=op1), r, w)

    def cp(self, eng, out, in_, r, w):
        if eng == "act":
            self.P.op("act", lambda e: e.activation(out=out, in_=in_, func=AF.Copy), r, w)
        else:
            self.P.op(eng, lambda e: e.tensor_copy(out=out, in_=in_), r, w)

    def mm(self, out, lhsT, rhs, start, stop, r, w):
        w = [(b, True) for b in w]
        self.P.op("pe", lambda e: e.matmul(out, lhsT=lhsT, rhs=rhs, start=start, stop=stop), r, w)

    def tr(self, out, in_, ident, r, w):
        w = [(b, True) for b in w]
        self.P.op("pe", lambda e: e.transpose(out=out, in_=in_, identity=ident), r, w)

    def ld(self, out, in_, slot, r=(), eng="sp"):
        self.P.dma(eng, lambda e: e.dma_start(out=out, in_=in_), list(r), [slot], slot)

    def stor(self, out, in_, slot, dbuf, eng="pool"):
        self.P.dma(eng, lambda e: e.dma_start(out=out, in_=in_), [slot], [(dbuf, True)], slot)

    def arena_reset(self, base=None):
        self.aoff = self.abase if base is None else base

    def al(self, words, dtype=F32, parts=128):
        assert self.aoff + words <= self.AW, ("arena overflow", self.aoff, words, self.AW)
        v = self.arena[0:parts, self.aoff:self.aoff + words]
        self.aoff += words
        if dtype != F32:
            v = v.bitcast(dtype)
        return v

    def alb(self, n_bf16, parts=128):
        return self.al((n_bf16 + 1) // 2, BF, parts)

    def build(self):
        nc, st = self.nc, self.st
        dt = lambda n, s, d, k: nc.dram_tensor(n, s, d, kind=k).ap()
        I = {}
        I["x"] = dt("x", [T, D], F32, "ExternalInput")
        I["pos"] = dt("pos", [1, T], I32, "ExternalInput")
        I["norm_g"] = dt("norm_g", [DEPTH, 128, 8], F32, "ExternalInput")
        I["w_in"] = dt("w_in", [DEPTH, D, NIN], F32, "ExternalInput")
        I["convw"] = dt("convw", [DEPTH, BD, NB * 4], F32, "ExternalInput")
        I["chv"] = dt("chv", [DEPTH, BD, 4 * NB], F32, "ExternalInput")
        I["w_rg"] = dt("w_rg", [DEPTH, NB, BD, BD], F32, "ExternalInput")
        I["w_ig"] = dt("w_ig", [DEPTH, NB, BD, BD], F32, "ExternalInput")
        I["w_oa"] = dt("w_oa", [DEPTH, D, D], F32, "ExternalInput")
        I["w_or"] = dt("w_or", [DEPTH, DR, D], F32, "ExternalInput")
        I["w_o"] = dt("w_o", [DEPTH, D, D], F32, "ExternalInput")
        I["final_g"] = dt("final_g", [1, D], F32, "ExternalInput")
        I["cbf"] = dt("cbf", [128, 4 * 128], BF, "ExternalInput")
        I["cf"] = dt("cf", [128, 128 + 2 + NIT + 2], F32, "ExternalInput")
        out = dt("out", [T, D], F32, "ExternalOutput")
        self.I = I
        S = {}
        _dt = dt
        dt = lambda n, s_, d, k: _dt(n, s_, d, "ExternalInput" if (k == "Internal" and n[2:] in self.ext) else k)
        S["tabs"] = dt("s_tabs", [4, 128, T], F32, "Internal")
        S["x1"] = dt("s_x1", [T, D], F32, "Internal")
        S["qT"] = dt("s_qT", [8, 128, T], BF, "Internal")
        S["kT"] = dt("s_kT", [8, 128, T], BF, "Internal")
        S["iqT"] = dt("s_iqT", [8, 128, T], BF, "Internal")
        S["ikT"] = dt("s_ikT", [128, T], BF, "Internal")
        S["v"] = dt("s_v", [T, D], BF, "Internal")
        S["sgaT"] = dt("s_sgaT", [8, 128, T], BF, "Internal")
        S["iw"] = dt("s_iw", [T, 16], F32, "Internal")
        S["smaT"] = dt("s_smaT", [8, 128, T], F32, "Internal")
        S["smbT"] = dt("s_smbT", [8, 128, T], F32, "Internal")
        S["ybT"] = dt("s_ybT", [NB, BD, T], BF, "Internal")
        S["gaoT"] = dt("s_gaoT", [8, 128, T], BF, "Internal")
        S["maskT"] = dt("s_maskT", [8, 128, 32 * 512], BF, "Internal")
        self.S = S
        dbg = {}
        if self.debug:
            for name in self.debug:
                src = S[name]
                dbg[name] = dt("dbg_" + name, list(src.shape), src.dtype, "ExternalOutput")
        self.dbg = dbg

        with st:
            self.AW = 47 * 1024
            self.arena = st.enter_context(nc.sbuf_tensor("arena", [128, self.AW], F32))
            self.pb = [st.enter_context(nc.psum_tensor("pb%d" % i, [128, 512], F32)) for i in range(8)]
            self.PB = [Buf("pb%d" % i) for i in range(8)]
            self.P = Prog(nc)
            P = self.P
            self.dram = Buf("dram")

            self.aoff = 0
            cb = self.alb(512)
            cf = self.al(128 + 2 + NIT + 2)
            self.abase = self.aoff
            self.Bc = Buf("const")
            self.ld(cb, I["cbf"][:, :], self.Bc)
            self.Bc2 = Buf("const2")
            self.ld(cf, I["cf"][:, :], self.Bc2)
            self.ident = cb[:, 0:128]
            self.Rattn = cb[:, 128:256]
            self.Ridx = cb[:, 256:384]
            self.ones = cb[:, 384:512]
            self.causal = cf[:, 0:128]
            self.invA = cf[:, 128:129]
            self.invI = cf[:, 129:130]
            self.pow2 = cf[:, 130:130 + NIT + 1]
            self.CB = [self.Bc, self.Bc2]

            ph = self.phases
            if "0" in ph:
                self.phase0()
            for l in range(self.nlayers):
                xin = I["x"] if l == 0 else S["x1"]
                last = (l == self.nlayers - 1)
                if "A" in ph:
                    self.phaseAB(l, xin)
                if "C" in ph:
                    self.phaseC(l)
                if "D1" in ph:
                    self.phaseD1(l)
                if "D2" in ph:
                    self.phaseD2(l)
                if "E" in ph:
                    self.phaseE(l, xin, out if last else S["x1"], last)
            P.barrier()
            for name, dst in dbg.items():
                b = Buf("dbg" + name)
                src = S[name]
                if len(src.shape) == 3:
                    for i in range(src.shape[0]):
                        self.P.dma("sp", (lambda s_, d_: (lambda e: e.dma_start(out=d_, in_=s_)))(src[i], dst[i]), [], [b], b)
                else:
                    self.P.dma("sp", (lambda s_, d_: (lambda e: e.dma_start(out=d_, in_=s_)))(src, dst), [], [b], b)
            P.barrier()
            P.emit(st)
        return nc

    def phase0(self):
        P, I, S = self.P, self.I, self.S
        self.arena_reset()
        posi = self.al(T, I32)
        posf = self.al(T)
        ang = self.al(T)
        t1 = self.al(T)
        t2 = self.al(T)
        t3 = self.al(T)
        Bp, Bf_, Ba, B1, B2, B3 = (Buf(n) for n in ("posi", "posf", "ang", "t1", "t2", "t3"))
        self.ld(posi, I["pos"][0:1, :].broadcast_to([128, T]), Bp)
        self.cp("dve", posf, posi, [Bp], [Bf_])
        TWO_PI = 2.0 * np.pi
        C1 = 6.28125
        C2 = TWO_PI - C1
        PI = float(np.pi)
        for ti, inv in enumerate((self.invA, self.invI)):
            self.ts("dve", ang, posf, inv, None, ALU.mult, None, [Bf_] + self.CB, [Ba])
            self.ts("dve", t1, ang, 1.0 / TWO_PI, None, ALU.mult, None, [Ba], [B1])
            ki = t2.bitcast(I32)
            self.cp("dve", ki, t1, [B1], [B2])
            self.cp("dve", t1, ki, [B2], [B1])
            self.stt(t2, t1, -C1, ang, ALU.mult, ALU.add, [B1, Ba], [B2])
            self.stt(t2, t1, -C2, t2, ALU.mult, ALU.add, [B1, B2], [B2])
            for which in (0, 1):
                if which == 0:
                    self.ts("dve", t3, t2, PI / 2, None, ALU.add, None, [B2], [B3])
                    src = t3
                    Bs = B3
                else:
                    src = t2
                    Bs = B2
                self.ts("dve", t1, src, PI, -TWO_PI, ALU.is_gt, ALU.mult, [Bs], [B1])
                self.tt("dve", t3, src, t1, ALU.add, [Bs, B1], [B3])
                self.ts("dve", t3, t3, -PI, PI, ALU.max, ALU.min, [B3], [B3])
                self.act(t3, t3, AF.Sin, [B3], [B3])
                self.stor(S["tabs"][2 * ti + which], t3, B3, self.dram)
        P.barrier()

    def phaseAB(self, l, xin):
        P, I, S = self.P, self.I, self.S
        P.new_epoch()
        self.arena_reset()
        hT = self.alb(8 * T).rearrange("p (c t) -> p c t", c=8)
        BhT = [Buf("hT%d" % i) for i in range(8)]
        self.hT, self.BhT = hT, BhT
        base_after_hT = self.aoff
        g = self.al(8)
        Bg = Buf("g")
        self.ld(g, I["norm_g"][l], Bg)
        xs = [self.al(D) for _ in range(4)]
        Bx = [Buf("x%d" % i) for i in range(4)]
        xn = [self.alb(D) for _ in range(4)]
        Bxn = [Buf("xn%d" % i) for i in range(4)]
        junk = self.al(D)
        Bj = Buf("junk")
        sm = self.al(8)
        Bs_ = Buf("ss")
        for tt_ in range(NT):
            s = tt_ % 4
            self.ld(xs[s], xin[tt_ * 128:(tt_ + 1) * 128, :], Bx[s])
            ss = sm[:, 0:1]
            self.act(junk, xs[s], AF.Square, [Bx[s]], [Bj, Bs_], accum_out=ss)
            self.ts("dve", sm[:, 1:2], ss, 1.0 / D, EPS, ALU.mult, ALU.add, [Bs_], [Bs_])
            self.act(sm[:, 2:3], sm[:, 1:2], AF.Sqrt, [Bs_], [Bs_])
            self.P.op("dve", (lambda o, i_: (lambda e: e.reciprocal(out=o, in_=i_)))(sm[:, 3:4], sm[:, 2:3]), [Bs_], [Bs_])
            self.act(xn[s], xs[s], AF.Copy, [Bx[s], Bs_], [Bxn[s]], scale=sm[:, 3:4])
            for half in range(2):
                pbi = (tt_ * 2 + half) % 2
                pv = self.pb[pbi][:].bitcast(BF)
                for c4 in range(4):
                    c = half * 4 + c4
                    self.tr(pv[:, c4 * 128:(c4 + 1) * 128], xn[s][:, c * 128:(c + 1) * 128], self.ident,
                            [Bxn[s], self.Bc], [self.PB[pbi]])
                for c4 in range(4):
                    c = half * 4 + c4
                    self.ts("dve", hT[:, c, tt_ * 128:(tt_ + 1) * 128], pv[:, c4 * 128:(c4 + 1) * 128],
                            g[:, c:c + 1], None, ALU.mult, None, [self.PB[pbi], Bg], [(BhT[tt_ // 4], True)])
        if self.stop == "A":
            dh = self.nc.dram_tensor("dbg_hT", [128, 8 * T], BF, kind="ExternalOutput").ap()
            bb = Buf("dbghT")
            self.P.dma("sp", lambda e: e.dma_start(out=dh, in_=hT.rearrange("p c t -> p (c t)")), BhT, [bb], bb)
            self.base_after_hT = base_after_hT
            return
        if "B" not in self.phases:
            self.base_after_hT = base_after_hT
            return
        P.barrier()
        self.arena_reset(base_after_hT)
        tabc = self.al(T)
        tabs_ = self.al(T)
        Btab = Buf("tab")
        wst = [self.al(8 * 256).rearrange("p (c n) -> p c n", c=8) for _ in range(2)]
        Bwst = [Buf("wst%d" % i) for i in range(2)]
        wb = [self.alb(8 * 256).rearrange("p (c n) -> p c n", c=8) for _ in range(2)]
        Bwb = [Buf("wb%d" % i) for i in range(2)]
        qb_ = [self.alb(512) for _ in range(2)]
        Bqb = [Buf("qb%d" % i) for i in range(2)]
        t1 = [self.al(512) for _ in range(2)]
        Bt1 = [Buf("t1%d" % i) for i in range(2)]
        t2 = [self.al(512) for _ in range(2)]
        Bt2 = [Buf("t2%d" % i) for i in range(2)]
        ob = [self.alb(512) for _ in range(6)]
        Bob = [Buf("ob%d" % i) for i in range(6)]
        of = [self.al(512) for _ in range(6)]
        Bof = [Buf("of%d" % i) for i in range(6)]
        self.cnt_w = 0
        self.cnt_e = 0
        self.cnt_o = 0
        self.cnt_f = 0
        w_in = I["w_in"]

        def load_w(col0, ncols, dup=False):
            s = self.cnt_w % 2
            self.cnt_w += 1
            if dup:
                for hh in range(2):
                    self.ld(wst[s][:, :, hh * 64:(hh + 1) * 64],
                            w_in[l, :, col0:col0 + 64].rearrange("(c p) n -> p c n", p=128), Bwst[s])
                ncols = 128
            else:
                self.ld(wst[s][:, :, 0:ncols], w_in[l, :, col0:col0 + ncols].rearrange("(c p) n -> p c n", p=128), Bwst[s])
            self.cp("act", wb[s][:, :, 0:ncols], wst[s][:, :, 0:ncols], [Bwst[s]], [Bwb[s]])
            return wb[s], Bwb[s]

        def fm_block(W, BW, m0, kind, dst, tab_loaded):
            rope = kind in ("ropeA", "ropeI")
            R = self.Rattn if kind == "ropeA" else self.Ridx
            pend = {}

            def proj(tb):
                pbi = self.cnt_e % 3
                self.cnt_e += 1
                ps = self.pb[pbi]
                for c in range(8):
                    self.mm(ps[:, :], W[:, c, m0:m0 + 128], hT[:, c, tb * 512:(tb + 1) * 512], c == 0, c == 7,
                            [BW, BhT[tb]], [self.PB[pbi]])
                tsl = slice(tb * 512, (tb + 1) * 512)
                if rope:
                    s = self.cnt_o % 2
                    so = self.cnt_o % 6
                    self.cnt_o += 1
                    self.cp("act", qb_[s], ps[:, :], [self.PB[pbi]], [Bqb[s]])
                    self.tt("dve", t1[s], ps[:, :], tabc[:, tsl], ALU.mult, [self.PB[pbi], Btab, Bqb[s]], [Bt1[s]])
                    pend[tb] = (s, so)
                elif kind == "copy":
                    so = self.cnt_o % 6
                    self.cnt_o += 1
                    self.cp("act", ob[so], ps[:, :], [self.PB[pbi]], [Bob[so]])
                    self.stor(dst[:, tsl], ob[so], Bob[so], self.dram)
                elif kind == "silu":
                    so = self.cnt_o % 6
                    self.cnt_o += 1
                    self.act(ob[so], ps[:, :], AF.Silu, [self.PB[pbi]], [Bob[so]])
                    self.stor(dst[:, tsl], ob[so], Bob[so], self.dram)
                elif kind == "sigm":
                    so = self.cnt_f % 6
                    self.cnt_f += 1
                    self.act(of[so], ps[:, :], AF.Sigmoid, [self.PB[pbi]], [Bof[so]])
                    self.stor(dst[:, tsl], of[so], Bof[so], self.dram)

            def ropef(tb):
                s, so = pend.pop(tb)
                tsl = slice(tb * 512, (tb + 1) * 512)
                pr = 3 + (so % 2)
                self.mm(self.pb[pr][:, :], R, qb_[s], True, True, [self.Bc, Bqb[s]], [self.PB[pr]])
                self.tt("dve", t2[s], self.pb[pr][:, :], tabs_[:, tsl], ALU.mult, [self.PB[pr], Btab], [Bt2[s]])
                self.tt("pool", ob[so], t1[s], t2[s], ALU.add, [Bt1[s], Bt2[s]], [Bob[so]])
                self.stor(dst[:, tsl], ob[so], Bob[so], self.dram)

            for i in range(9):
                if i < 8:
                    proj(i)
                if rope and i >= 1:
                    ropef(i - 1)

        def load_tabs(ti):
            self.ld(tabc, S["tabs"][2 * ti], Btab)
            self.ld(tabs_, S["tabs"][2 * ti + 1], Btab)

        if self.stop == "B1":
            load_tabs(0)
            W, BW = load_w(O_Q, 256)
            dtab = self.nc.dram_tensor("dbg_tab", [2, 128, T], F32, kind="ExternalOutput").ap()
            bb = Buf("dbgtab")
            self.P.dma("sp", lambda e: e.dma_start(out=dtab[0], in_=tabc), [Btab], [bb], bb)
            self.P.dma("sp", lambda e: e.dma_start(out=dtab[1], in_=tabs_), [Btab], [bb], bb)
            fm_block(W, BW, 0, "copy", S["kT"][0], True)
            fm_block(W, BW, 0, "ropeA", S["qT"][0], True)
            W, BW = load_w(O_IW, 16)
            for tt_ in range(2):
                pbi = self.cnt_e % 3
                self.cnt_e += 1
                ps = self.pb[pbi]
                for c in range(8):
                    self.mm(ps[:, 0:16], hT[:, c, tt_ * 128:(tt_ + 1) * 128], W[:, c, 0:16], c == 0, c == 7,
                            [BW, BhT[tt_ // 4]], [self.PB[pbi]])
                so = self.cnt_f % 6
                self.cnt_f += 1
                self.cp("act", of[so][:, 0:16], ps[:, 0:16], [self.PB[pbi]], [Bof[so]])
                self.stor(S["iw"][tt_ * 128:(tt_ + 1) * 128, :], of[so][:, 0:16], Bof[so], self.dram)
            self.base_after_hT = base_after_hT
            return
        load_tabs(0)
        for grp, off, dst in (("q", O_Q, S["qT"]), ("k", O_K, S["kT"])):
            for cbk in range(4):
                W, BW = load_w(off + cbk * 256, 256)
                for m in range(2):
                    fm_block(W, BW, m * 128, "ropeA", dst[cbk * 2 + m], True)
        load_tabs(1)
        for cbk in range(4):
            W, BW = load_w(O_IQ + cbk * 256, 256)
            for m in range(2):
                fm_block(W, BW, m * 128, "ropeI", S["iqT"][cbk * 2 + m], True)
        W, BW = load_w(O_IK, 64, dup=True)
        fm_block(W, BW, 0, "ropeI", S["ikT"], True)
        for cbk in range(4):
            W, BW = load_w(O_GA + cbk * 256, 256)
            for m in range(2):
                fm_block(W, BW, m * 128, "silu", S["sgaT"][cbk * 2 + m], False)
        for off, dst in ((O_MA, S["smaT"]), (O_MB, S["smbT"])):
            for cbk in range(4):
                W, BW = load_w(off + cbk * 256, 256)
                for m in range(2):
                    fm_block(W, BW, m * 128, "sigm", dst[cbk * 2 + m], False)
        for cbk in range(4):
            W, BW = load_w(O_V + cbk * 256, 256)
            for tt_ in range(NT):
                pbi = self.cnt_e % 3
                self.cnt_e += 1
                ps = self.pb[pbi]
                for c in range(8):
                    self.mm(ps[:, 0:256], hT[:, c, tt_ * 128:(tt_ + 1) * 128], W[:, c, 0:256], c == 0, c == 7,
                            [BW, BhT[tt_ // 4]], [self.PB[pbi]])
                so = self.cnt_o % 6
                self.cnt_o += 1
                self.cp("act", ob[so][:, 0:256], ps[:, 0:256], [self.PB[pbi]], [Bob[so]])
                self.stor(S["v"][tt_ * 128:(tt_ + 1) * 128, cbk * 256:(cbk + 1) * 256], ob[so][:, 0:256], Bob[so], self.dram)
        W, BW = load_w(O_IW, 16)
        for tt_ in range(NT):
            pbi = self.cnt_e % 3
            self.cnt_e += 1
            ps = self.pb[pbi]
            for c in range(8):
                self.mm(ps[:, 0:16], hT[:, c, tt_ * 128:(tt_ + 1) * 128], W[:, c, 0:16], c == 0, c == 7,
                        [BW, BhT[tt_ // 4]], [self.PB[pbi]])
            so = self.cnt_f % 6
            self.cnt_f += 1
            self.cp("act", of[so][:, 0:16], ps[:, 0:16], [self.PB[pbi]], [Bof[so]])
            self.stor(S["iw"][tt_ * 128:(tt_ + 1) * 128, :], of[so][:, 0:16], Bof[so], self.dram)
        self.base_after_hT = base_after_hT

    def phaseC(self, l):
        P, I, S = self.P, self.I, self.S
        hT, BhT = self.hT, self.BhT
        P.barrier()
        self.arena_reset(self.base_after_hT)
        w_in = I["w_in"]
        cw = self.al(NB * 4, parts=BD).rearrange("p (n k) -> p n k", k=4)
        chv = self.al(4 * NB, parts=BD).rearrange("p (v n) -> p v n", v=4)
        Bcw = Buf("cw")
        Bchv = Buf("chv")
        self.ld(cw, I["convw"][l].rearrange("p (n k) -> p n k", k=4), Bcw)
        self.ld(chv, I["chv"][l].rearrange("p (v n) -> p v n", v=4), Bchv)
        clam = self.al(2 * NB, parts=BD).rearrange("p (v n) -> p v n", v=2)
        Bcl = Buf("clam")
        tmpc = self.al(NB, parts=BD)
        Btc = Buf("tmpc")
        self.act(tmpc, chv[:, 3, :], AF.Exp, [Bchv], [Btc], scale=-1.0)
        self.act(tmpc, tmpc, AF.Ln, [Btc], [Btc], bias=1.0)
        self.ts("dve", clam[:, 0, :], tmpc, -8.0, None, ALU.mult, None, [Btc], [Bcl])
        self.ts("dve", clam[:, 1, :], tmpc, -16.0, None, ALU.mult, None, [Btc], [Bcl])
        gst = self.al(NB * BD, parts=BD).rearrange("p (n d) -> p n d", n=NB)
        Bgst = Buf("gst")
        wr = self.alb(NB * BD, parts=BD).rearrange("p (n d) -> p n d", n=NB)
        wi = self.alb(NB * BD, parts=BD).rearrange("p (n d) -> p n d", n=NB)
        Bwr, Bwi = Buf("wr"), Buf("wi")
        self.ld(gst, I["w_rg"][l].rearrange("n c d -> c n d"), Bgst)
        self.cp("act", wr, gst, [Bgst], [Bwr])
        self.ld(gst, I["w_ig"][l].rearrange("n c d -> c n d"), Bgst)
        self.cp("act", wi, gst, [Bgst], [Bwi])
        wst = [self.al(8 * 2 * BD).rearrange("p (c n) -> p c n", c=8) for _ in range(2)]
        Bwst = [Buf("cwst%d" % i) for i in range(2)]
        wxg = [self.alb(8 * 2 * BD).rearrange("p (c n) -> p c n", c=8) for _ in range(2)]
        Bwxg = [Buf("wxg%d" % i) for i in range(2)]
        xrb = [self.al(515, parts=BD) for _ in range(3)]
        Bxrb = [Buf("xrb%d" % i) for i in range(3)]
        xc = [self.al(512, parts=BD) for _ in range(3)]
        Bxc = [Buf("xc%d" % i) for i in range(3)]
        xcb = [self.alb(512, parts=BD) for _ in range(2)]
        Bxcb = [Buf("xcb%d" % i) for i in range(2)]
        rr = [self.al(512, parts=BD) for _ in range(2)]
        Brr = [Buf("rr%d" % i) for i in range(2)]
        ii = [self.al(512, parts=BD) for _ in range(2)]
        Bii = [Buf("ii%d" % i) for i in range(2)]
        aa = [self.al(512, parts=BD) for _ in range(2)]
        Baa = [Buf("aa%d" % i) for i in range(2)]
        sq = [self.al(512, parts=BD) for _ in range(2)]
        Bsq = [Buf("sq%d" % i) for i in range(2)]
        uu = [self.al(512, parts=BD) for _ in range(2)]
        Buu = [Buf("uu%d" % i) for i in range(2)]
        hh = [self.al(512, parts=BD) for _ in range(2)]
        Bhh = [Buf("hh%d" % i) for i in range(2)]
        sg = [self.al(512, parts=BD) for _ in range(2)]
        Bsg = [Buf("sg%d" % i) for i in range(2)]
        yb = [self.alb(512, parts=BD) for _ in range(3)]
        Byb = [Buf("yb%d" % i) for i in range(3)]
        hb = self.al(3 * NB, parts=BD).rearrange("p (v n) -> p v n", v=3)
        Bhb = Buf("hb")
        self.ts("dve", hb[:, 0:2, :], chv[:, 1:3, :], 0.5, None, ALU.mult, None, [Bchv], [Bhb])
        self.ts("dve", hb[:, 2, :], clam[:, 0, :], 0.5, None, ALU.mult, None, [Bcl], [Bhb])
        ctmp = self.al(512, parts=BD)
        Bctmp = Buf("ctmp")
        nbk = self.lim.get('C_nb', NB)
        iters = [(n, tb) for n in range(nbk) for tb in range(8)]

        def Sa_pe(it):
            n, tb = iters[it]
            ws = n % 2
            s3 = it % 3
            pA = it % 2
            tsl = slice(tb * 512, (tb + 1) * 512)
            def loadw(nn):
                w2 = nn % 2
                self.ld(wst[w2][:, :, 0:BD], w_in[l, :, O_XR + nn * BD:O_XR + (nn + 1) * BD].rearrange("(c p) n -> p c n", p=128), Bwst[w2])
                self.ld(wst[w2][:, :, BD:2 * BD], w_in[l, :, O_GR + nn * BD:O_GR + (nn + 1) * BD].rearrange("(c p) n -> p c n", p=128), Bwst[w2])
                self.cp("act", wxg[w2], wst[w2], [Bwst[w2]], [Bwxg[w2]])
            if it == 0:
                loadw(0)
            if tb == 4 and n + 1 < nbk:
                loadw(n + 1)
            for c in range(8):
                self.mm(self.pb[pA][0:BD, :], wxg[ws][:, c, 0:BD], hT[:, c, tsl], c == 0, c == 7,
                        [Bwxg[ws], BhT[tb]], [self.PB[pA]])

        def Sa_rest(it):
            n, tb = iters[it]
            s3 = it % 3
            pA = it % 2
            if tb == 0:
                self.P.op("pool", (lambda o: (lambda e: e.memset(o, 0.0)))(xrb[s3][:, 0:3]), [], [Bxrb[s3]])
            else:
                sp = (it - 1) % 3
                self.cp("pool", xrb[s3][:, 0:3], xrb[sp][:, 512:515], [Bxrb[sp]], [Bxrb[s3]])
            self.cp("act", xrb[s3][:, 3:515], self.pb[pA][0:BD, :], [self.PB[pA]], [Bxrb[s3]])
            self.ts("pool", xc[s3], xrb[s3][:, 3:515], cw[:, n, 3:4], chv[:, 0, n:n + 1], ALU.mult, ALU.add,
                    [Bxrb[s3], Bcw, Bchv], [Bxc[s3]])
            self.ts("pool", ctmp, xrb[s3][:, 2:514], cw[:, n, 2:3], 0.0, ALU.mult, ALU.add, [Bxrb[s3], Bcw], [Bctmp])
            self.tt("pool", xc[s3], xc[s3], ctmp, ALU.add, [Bxc[s3], Bctmp], [Bxc[s3]])

        def Sb(it):
            n, tb = iters[it]
            ws = n % 2
            s3 = it % 3
            s = it % 2
            pB, pC, pD = 2 + s, 4 + s, 6 + s
            tsl = slice(tb * 512, (tb + 1) * 512)
            for k in (1, 0):
                self.stt(xc[s3], xrb[s3][:, k:k + 512], cw[:, n, k:k + 1], xc[s3], ALU.mult, ALU.add,
                         [Bxrb[s3], Bcw, Bxc[s3]], [Bxc[s3]])
            self.cp("pool", xcb[s], xc[s3], [Bxc[s3]], [Bxcb[s]])
            for c in range(8):
                self.mm(self.pb[pD][0:BD, :], wxg[ws][:, c, BD:2 * BD], hT[:, c, tsl], c == 0, c == 7,
                        [Bwxg[ws], BhT[tb]], [self.PB[pD]])

        def Sg(it):
            n, tb = iters[it]
            s = it % 2
            pB, pC = 2 + s, 4 + s
            self.mm(self.pb[pB][0:BD, :], wr[:, n, :], xcb[s], True, True, [Bwr, Bxcb[s]], [self.PB[pB]])
            self.mm(self.pb[pC][0:BD, :], wi[:, n, :], xcb[s], True, True, [Bwi, Bxcb[s]], [self.PB[pC]])

        def Sc(it):
            n, tb = iters[it]
            s3 = it % 3
            s = it % 2
            pB, pC, pD = 2 + s, 4 + s, 6 + s
            self.act(rr[s], self.pb[pB][0:BD, :], AF.Tanh, [self.PB[pB], Bhb], [Brr[s]], scale=0.5, bias=hb[:, 0, n:n + 1])
            self.act(ii[s], self.pb[pC][0:BD, :], AF.Tanh, [self.PB[pC], Bhb], [Bii[s]], scale=0.5, bias=hb[:, 1, n:n + 1])
            self.act(sg[s], self.pb[pD][0:BD, :], AF.Tanh, [self.PB[pD]], [Bsg[s]], scale=0.5)
            self.act(aa[s], rr[s], AF.Exp, [Brr[s], Bhb], [Baa[s]], scale=hb[:, 2, n:n + 1], bias=hb[:, 2, n:n + 1])
            self.act(sq[s], rr[s], AF.Exp, [Brr[s], Bcl], [Bsq[s]], scale=clam[:, 0, n:n + 1], bias=clam[:, 0, n:n + 1])
            self.act(sq[s], sq[s], AF.Ln, [Bsq[s]], [Bsq[s]], scale=-1.0, bias=1.0)
            self.act(sq[s], sq[s], AF.Exp, [Bsq[s]], [Bsq[s]], scale=0.5)
            self.stt(uu[s], ii[s], 1.0, xc[s3], ALU.add, ALU.mult, [Bii[s], Bxc[s3]], [Buu[s]])
            self.stt(sg[s], sg[s], 1.0, self.pb[pD][0:BD, :], ALU.add, ALU.mult, [Bsg[s], self.PB[pD]], [Bsg[s]])

        def Sd(it):
            n, tb = iters[it]
            s = it % 2
            sy = it % 3
            tsl = slice(tb * 512, (tb + 1) * 512)
            self.stt(uu[s], uu[s], 0.5, sq[s], ALU.mult, ALU.mult, [Buu[s], Bsq[s]], [Buu[s]])
            init = 0.0 if tb == 0 else hh[1 - s][:, 511:512]
            rd = [Baa[s], Buu[s]] + ([] if tb == 0 else [Bhh[1 - s]])
            self.P.op("dve", (lambda o, a_, u_, i_: (lambda e: e.tensor_tensor_scan(out=o, data0=a_, data1=u_, initial=i_, op0=ALU.mult, op1=ALU.add)))(hh[s], aa[s], uu[s], init), rd, [Bhh[s]])
            self.stt(yb[sy], hh[s], 0.5, sg[s], ALU.mult, ALU.mult, [Bhh[s], Bsg[s]], [Byb[sy]])
            self.stor(S["ybT"][n][:, tsl], yb[sy], Byb[sy], self.dram, eng="sp")

        NI = len(iters)
        for t in range(-3, NI + 1):
            if 0 <= t < NI:
                Sg(t)
            if 0 <= t + 2 < NI:
                Sa_rest(t + 2)
            if 0 <= t + 1 < NI:
                Sb(t + 1)
            if 0 <= t < NI:
                Sc(t)
            if 0 <= t - 1 < NI:
                Sd(t - 1)
            if 0 <= t + 3 < NI:
                Sa_pe(t + 3)

    def phaseD1(self, l):
        P, I, S = self.P, self.I, self.S
        P.new_epoch()
        self.arena_reset()
        ikT = self.alb(T)
        Bik = Buf("ikT")
        self.ld(ikT, S["ikT"][:, :], Bik)
        iqb = [self.alb(8 * 512).rearrange("p (c t) -> p c t", c=8) for _ in range(2)]
        Biq = [Buf("iqb%d" % i) for i in range(2)]
        iwb = [self.al(64).rearrange("p (t h) -> p t h", t=4) for _ in range(2)]
        Biw = [Buf("iwb%d" % i) for i in range(2)]
        dg = [self.alb(16 * 128).rearrange("p (h q) -> p h q", h=16) for _ in range(2)]
        Bdg = [Buf("dg%d" % i) for i in range(2)]
        rl = [self.alb(512) for _ in range(4)]
        Brl = [Buf("rl%d" % i) for i in range(4)]
        sc = [self.al(T) for _ in range(2)]
        Bsc = [Buf("sc%d" % i) for i in range(2)]
        junk = self.alb(T)
        Bj = Buf("junkb")
        mrow = [self.alb(T) for _ in range(2)]
        Bmr = [Buf("mrow%d" % i) for i in range(2)]
        mT = [self.alb(32 * 512).rearrange("p (k q) -> p k q", k=32) for _ in range(2)]
        BmT = [Buf("mT%d" % i) for i in range(2)]
        sm = [self.al(16 + NIT + 1) for _ in range(2)]
        Bsm = [Buf("sm%d" % i) for i in range(2)]
        st_ = {"nrl": 0, "npd": 0}
        nqb = self.lim.get('D1_qb', 8)

        def S1(j):
            qb, jj = divmod(j, 4)
            qs = qb % 2
            js = j % 2
            nk = (j + 1) * 128
            if jj == 0:
                self.ld(iqb[qs], S["iqT"][:, :, qb * 512:(qb + 1) * 512].rearrange("c p t -> p c t"), Biq[qs])
                self.ld(iwb[qs], S["iw"][qb * 512:(qb + 1) * 512, :].rearrange("(t p) h -> p t h", p=128), Biw[qs])
            for h in range(16):
                self.act(dg[js][:, h, :], self.ident, AF.Copy, [self.Bc, Biw[qs]], [Bdg[js]], scale=iwb[qs][:, jj, h:h + 1])
            nch = (nk + 511) // 512
            items = [(c, h) for c in range(nch) for h in range(16)]
            slots = {}

            def dots(c, h):
                w = min(512, nk - c * 512)
                pd = st_["npd"] % 4
                st_["npd"] += 1
                r0 = (h % 2) * 64
                self.mm(self.pb[pd][:, 0:w], iqb[qs][r0:r0 + 64, h // 2, jj * 128:(jj + 1) * 128],
                        ikT[r0:r0 + 64, c * 512:c * 512 + w], True, True, [Biq[qs], Bik], [self.PB[pd]])
                rs = st_["nrl"] % 4
                st_["nrl"] += 1
                self.act(rl[rs][:, 0:w], self.pb[pd][:, 0:w], AF.Relu, [self.PB[pd]], [Brl[rs]])
                slots[(c, h)] = rs

            def acc(c, h):
                w = min(512, nk - c * 512)
                pacc = 6 + (c % 2)
                rs = slots.pop((c, h))
                self.mm(self.pb[pacc][:, 0:w], dg[js][:, h, :], rl[rs][:, 0:w], h == 0, h == 15,
                        [Bdg[js], Brl[rs]], [self.PB[pacc]])
                if h == 15:
                    self.cp("act", sc[js][:, c * 512:c * 512 + w], self.pb[pacc][:, 0:w], [self.PB[pacc]], [Bsc[js]])

            npair = len(items) // 2
            for pi in range(npair + 1):
                if pi < npair:
                    dots(*items[2 * pi])
                    dots(*items[2 * pi + 1])
                if pi >= 1:
                    acc(*items[2 * pi - 2])
                    acc(*items[2 * pi - 1])

        def S2(j):
            js = j % 2
            nk = (j + 1) * 128
            s_ = sm[js]
            B_ = Bsm[js]
            diag = sc[js][:, j * 128:(j + 1) * 128]
            if j >= 2:
                self.P.op("dve", (lambda o, i_: (lambda e: e.tensor_reduce(out=o, in_=i_, axis=AX.X, op=ALU.max)))(s_[:, 0:1], sc[js][:, 0:nk]), [Bsc[js]], [B_])
                self.P.op("dve", (lambda o, i_: (lambda e: e.tensor_reduce(out=o, in_=i_, axis=AX.X, op=ALU.min)))(s_[:, 1:2], sc[js][:, 0:nk]), [Bsc[js]], [B_])
                self.tt("dve", diag, diag, self.causal, ALU.add, [Bsc[js], self.Bc2], [Bsc[js]])
                self.tt("dve", s_[:, 2:3], s_[:, 0:1], s_[:, 1:2], ALU.subtract, [B_], [B_])
                steps = s_[:, 16:16 + NIT + 1]
                self.ts("dve", steps, self.pow2, s_[:, 2:3], None, ALU.mult, None, [self.Bc2, B_], [B_])
                self.tt("dve", s_[:, 3:4], s_[:, 1:2], steps[:, 0:1], ALU.add, [B_], [B_])
                for k in range(NIT):
                    self.ts("dve", junk[:, 0:nk], sc[js][:, 0:nk], s_[:, 3:4], 0.0, ALU.is_ge, ALU.add,
                            [Bsc[js], B_], [Bj, B_], accum_out=s_[:, 4:5])
                    self.ts("dve", s_[:, 5:6], s_[:, 4:5], float(TOPK) - 0.5, -0.5, ALU.is_ge, ALU.add, [B_], [B_])
                    self.stt(s_[:, 3:4], s_[:, 5:6], steps[:, k:k + 1], s_[:, 3:4], ALU.mult, ALU.add, [B_], [B_])
                self.tt("dve", s_[:, 6:7], s_[:, 3:4], steps[:, NIT:NIT + 1], ALU.subtract, [B_], [B_])
            else:
                self.tt("dve", diag, diag, self.causal, ALU.add, [Bsc[js], self.Bc2], [Bsc[js]])
                self.P.op("dve", (lambda o: (lambda e: e.memset(o, -1e29)))(s_[:, 6:7]), [], [B_])
            self.ts("dve", mrow[js][:, 0:nk], sc[js][:, 0:nk], s_[:, 6:7], None, ALU.is_ge, None,
                    [Bsc[js], B_], [Bmr[js]])

        def S3(j):
            qb, jj = divmod(j, 4)
            qs = qb % 2
            js = j % 2
            for k0 in range(0, j + 1, 4):
                kn = min(4, j + 1 - k0)
                pt = 4 + ((k0 // 4) % 2)
                pv = self.pb[pt][:].bitcast(BF)
                for kk in range(kn):
                    kt = k0 + kk
                    self.tr(pv[:, kk * 128:(kk + 1) * 128], mrow[js][:, kt * 128:(kt + 1) * 128], self.ident,
                            [Bmr[js], self.Bc], [self.PB[pt]])
                self.cp("act", mT[qs][:, k0:k0 + kn, jj * 128:(jj + 1) * 128],
                        pv[:, 0:kn * 128].rearrange("p (k q) -> p k q", k=kn), [self.PB[pt]], [(BmT[qs], True)])
            if jj == 3:
                nkt = 4 * qb + 4
                self.stor(S["maskT"][qb][:, 0:nkt * 512].rearrange("p (k q) -> p k q", k=nkt), mT[qs][:, 0:nkt, :], BmT[qs], self.dram, eng="sp")

        for j in range(nqb * 4):
            S1(j)
            if j > 0:
                S3(j - 1)
            S2(j)
        S3(nqb * 4 - 1)

    def phaseD2(self, l):
        P, I, S = self.P, self.I, self.S
        P.barrier()
        self.arena_reset()
        kT = [self.alb(T) for _ in range(2)]
        BkT = [Buf("kT%d" % i) for i in range(2)]
        vv = [self.alb(32 * 128).rearrange("p (t d) -> p t d", t=32) for _ in range(2)]
        Bvv = [Buf("vv%d" % i) for i in range(2)]
        mT = [self.alb(32 * 512).rearrange("p (k q) -> p k q", k=32) for _ in range(2)]
        BmT = [Buf("mT%d" % i) for i in range(2)]
        qT = [self.alb(512) for _ in range(2)]
        BqT = [Buf("qT%d" % i) for i in range(2)]
        sga = [self.alb(512) for _ in range(2)]
        Bsga = [Buf("sga%d" % i) for i in range(2)]
        pt = [self.alb(512) for _ in range(4)]
        Bpt = [Buf("pt%d" % i) for i in range(4)]
        lnb = [self.al(512) for _ in range(2)]
        Bln = [Buf("ln%d" % i) for i in range(2)]
        ot = [self.al(512) for _ in range(2)]
        Bot = [Buf("ot%d" % i) for i in range(2)]
        go = [self.alb(512) for _ in range(3)]
        Bgo = [Buf("go%d" % i) for i in range(3)]
        scale = float(128 ** -0.5)
        it = 0
        st2 = {"npt": 0, "nS": 0}
        for h in range(self.lim.get('D2_h', 8)):
            hs = h % 2
            self.ld(kT[hs], S["kT"][h], BkT[hs])
            for q4 in range(4):
                self.P.dma("sp", (lambda o, i_: (lambda e: e.dma_start(out=o, in_=i_)))(
                    vv[hs][:, q4 * 8:(q4 + 1) * 8, :],
                    S["v"][q4 * 1024:(q4 + 1) * 1024, h * 128:(h + 1) * 128].rearrange("(t p) d -> p t d", p=128)),
                    [], [(Bvv[hs], q4 > 0)], Bvv[hs])
            for qb in range(self.lim.get('D2_qb', 8)):
                s = it % 2
                s3 = it % 3
                it += 1
                nkt = 4 * qb + 4
                qsl = slice(qb * 512, (qb + 1) * 512)
                self.ld(mT[s][:, 0:nkt, :], S["maskT"][qb][:, 0:nkt * 512].rearrange("p (k q) -> p k q", k=nkt), BmT[s])
                self.ld(qT[s], S["qT"][h][:, qsl], BqT[s])
                self.ld(sga[s], S["sgaT"][h][:, qsl], Bsga[s])
                pO = 4 + s * 2
                pS_ = 5 + s * 2
                slots = {}

                def qk(kt, s=s, qb=qb, hs=hs):
                    i0 = max(0, kt - 4 * qb)
                    cs = slice(i0 * 128, 512)
                    pS = st2["nS"] % 4
                    st2["nS"] += 1
                    self.mm(self.pb[pS][:, cs], kT[hs][:, kt * 128:(kt + 1) * 128], qT[s][:, cs], True, True,
                            [BkT[hs], BqT[s]], [self.PB[pS]])
                    ps_ = st2["npt"] % 4
                    st2["npt"] += 1
                    self.act(pt[ps_][:, cs], self.pb[pS][:, cs], AF.Exp, [self.PB[pS]], [Bpt[ps_]], scale=scale)
                    self.tt("dve", pt[ps_][:, cs], pt[ps_][:, cs], mT[s][:, kt, cs], ALU.mult, [Bpt[ps_], BmT[s]], [Bpt[ps_]])
                    slots[kt] = (ps_, cs)

                def pv(kt, s=s, hs=hs, nkt=nkt, pO=pO, pS_=pS_):
                    ps_, cs = slots.pop(kt)
                    self.mm(self.pb[pO][:, cs], vv[hs][:, kt, :], pt[ps_][:, cs], kt == 0, kt == nkt - 1,
                            [Bvv[hs], Bpt[ps_]], [self.PB[pO]])
                    self.mm(self.pb[pS_][:, cs], self.ones, pt[ps_][:, cs], kt == 0, kt == nkt - 1,
                            [self.Bc, Bpt[ps_]], [self.PB[pS_]])

                LA = 2
                for i in range(nkt + LA):
                    if i < nkt:
                        qk(i)
                    if i >= LA:
                        pv(i - LA)
                self.act(lnb[s], self.pb[pS_][:, :], AF.Ln, [self.PB[pS_]], [Bln[s]])
                self.act(lnb[s], lnb[s], AF.Exp, [Bln[s]], [Bln[s]], scale=-1.0)
                self.tt("dve", ot[s], self.pb[pO][:, :], lnb[s], ALU.mult, [self.PB[pO], Bln[s]], [Bot[s]])
                self.tt("dve", go[s3], ot[s], sga[s], ALU.mult, [Bot[s], Bsga[s]], [Bgo[s3]])
                self.stor(S["gaoT"][h][:, qsl], go[s3], Bgo[s3], self.dram, eng="pool")

    def phaseE(self, l, xin, xout, last):
        P, I, S = self.P, self.I, self.S
        P.barrier()
        self.arena_reset()
        woa = self.alb(8 * D).rearrange("p (c n) -> p c n", c=8)
        NC_R = DR // 128
        wor = self.alb(NC_R * D).rearrange("p (c n) -> p c n", c=NC_R)
        wo = self.alb(8 * D).rearrange("p (c n) -> p c n", c=8)
        Bwoa, Bwor, Bwo = Buf("woa"), Buf("wor"), Buf("wo")
        wst = [self.al(D) for _ in range(2)]
        Bwst = [Buf("ewst%d" % i) for i in range(2)]
        nw = 0
        for c in range(8):
            s = nw % 2
            nw += 1
            self.ld(wst[s], I["w_oa"][l, c * 128:(c + 1) * 128, :], Bwst[s])
            self.cp("act", woa[:, c, :], wst[s], [Bwst[s]], [(Bwoa, True)])
        for c in range(NC_R):
            s = nw % 2
            nw += 1
            self.ld(wst[s], I["w_or"][l, c * 128:(c + 1) * 128, :], Bwst[s])
            self.cp("act", wor[:, c, :], wst[s], [Bwst[s]], [(Bwor, True)])
        for c in range(8):
            s = nw % 2
            nw += 1
            self.ld(wst[s], I["w_o"][l, c * 128:(c + 1) * 128, :], Bwst[s])
            self.cp("act", wo[:, c, :], wst[s], [Bwst[s]], [(Bwo, True)])
        fg = None
        if last:
            fg = self.al(D)
            Bfg = Buf("fg")
            self.ld(fg, I["final_g"][0:1, :].broadcast_to([128, D]), Bfg)
        gao = [self.alb(8 * 512).rearrange("p (c t) -> p c t", c=8) for _ in range(2)]
        Bgao = [Buf("gao%d" % i) for i in range(2)]
        ybb = [self.alb(NC_R * 512).rearrange("p (c t) -> p c t", c=NC_R) for _ in range(2)]
        Bybb = [Buf("ybb%d" % i) for i in range(2)]
        sma = [self.al(512) for _ in range(2)]
        Bsma = [Buf("sma%d" % i) for i in range(2)]
        smb = [self.al(512) for _ in range(2)]
        Bsmb = [Buf("smb%d" % i) for i in range(2)]
        t1 = [self.al(512) for _ in range(2)]
        Bt1 = [Buf("et1%d" % i) for i in range(2)]
        t2 = [self.al(512) for _ in range(2)]
        Bt2 = [Buf("et2%d" % i) for i in range(2)]
        mT = [self.alb(8 * 512).rearrange("p (c t) -> p c t", c=8) for _ in range(2)]
        BmT = [Buf("emT%d" % i) for i in range(2)]
        xs = [self.al(D) for _ in range(3)]
        Bxs = [Buf("exs%d" % i) for i in range(3)]
        xo = [self.al(D) for _ in range(3)]
        Bxo = [Buf("exo%d" % i) for i in range(3)]
        junk = self.al(D)
        Bj = Buf("ejunk")
        sm = self.al(8)
        io = 0
        ix = 0
        for tb in range(self.lim.get('E_tb', 8)):
            s = tb % 2
            tsl = slice(tb * 512, (tb + 1) * 512)
            self.ld(gao[s], S["gaoT"][:, :, tsl].rearrange("c p t -> p c t"), Bgao[s])
            self.ld(ybb[s], S["ybT"].rearrange("n c t -> (n c) t")[:, tsl].rearrange("(c p) t -> p c t", p=128), Bybb[s])
            for oc in range(8):
                so = io % 2
                io += 1
                self.ld(sma[so], S["smaT"][oc][:, tsl], Bsma[so])
                self.ld(smb[so], S["smbT"][oc][:, tsl], Bsmb[so])
                pA = (io % 2) * 2
                pB = pA + 1
                for c in range(8):
                    self.mm(self.pb[pA][:, :], woa[:, c, oc * 128:(oc + 1) * 128], gao[s][:, c, :], c == 0, c == 7,
                            [Bwoa, Bgao[s]], [self.PB[pA]])
                for c in range(NC_R):
                    self.mm(self.pb[pB][:, :], wor[:, c, oc * 128:(oc + 1) * 128], ybb[s][:, c, :], c == 0, c == NC_R - 1,
                            [Bwor, Bybb[s]], [self.PB[pB]])
                self.tt("dve", t1[so], self.pb[pA][:, :], sma[so], ALU.mult, [self.PB[pA], Bsma[so]], [Bt1[so]])
                self.tt("dve", t2[so], self.pb[pB][:, :], smb[so], ALU.mult, [self.PB[pB], Bsmb[so]], [Bt2[so]])
                self.tt("dve", mT[s][:, oc, :], t1[so], t2[so], ALU.add, [Bt1[so], Bt2[so]], [(BmT[s], True)])
            for tq in range(4):
                sx = ix % 3
                ix += 1
                tt_ = tb * 4 + tq
                self.ld(xs[sx], xin[tt_ * 128:(tt_ + 1) * 128, :], Bxs[sx])
                for half in range(2):
                    pO = 4 + (ix % 2) * 2 + half
                    for c in range(8):
                        self.mm(self.pb[pO][:, :], mT[s][:, c, tq * 128:(tq + 1) * 128], wo[:, c, half * 512:(half + 1) * 512],
                                c == 0, c == 7, [BmT[s], Bwo], [self.PB[pO]])
                    self.tt("dve", xo[sx][:, half * 512:(half + 1) * 512], self.pb[pO][:, :], xs[sx][:, half * 512:(half + 1) * 512],
                            ALU.add, [self.PB[pO], Bxs[sx]], [Bxo[sx]])
                if last:
                    Bs_ = Buf("ess")
                    ss = sm[:, 0:1]
                    self.act(junk, xo[sx], AF.Square, [Bxo[sx]], [Bj, Bs_], accum_out=ss)
                    self.ts("dve", sm[:, 1:2], ss, 1.0 / D, EPS, ALU.mult, ALU.add, [Bs_], [Bs_])
                    self.act(sm[:, 2:3], sm[:, 1:2], AF.Sqrt, [Bs_], [Bs_])
                    self.P.op("dve", (lambda o, i_: (lambda e: e.reciprocal(out=o, in_=i_)))(sm[:, 3:4], sm[:, 2:3]), [Bs_], [Bs_])
                    self.stt(xo[sx], xo[sx], sm[:, 3:4], fg, ALU.mult, ALU.mult, [Bxo[sx], Bs_, Bfg], [Bxo[sx]])
                self.stor(xout[tt_ * 128:(tt_ + 1) * 128, :], xo[sx], Bxo[sx], self.dram, eng="pool")


def _consts():
    ident = np.eye(128, dtype=np.float32)
    Ra = np.zeros((128, 128), np.float32)
    for dp in range(64):
        Ra[dp + 64, dp] = -1.0
        Ra[dp, dp + 64] = 1.0
    Ri = np.zeros((128, 128), np.float32)
    for hh in range(2):
        for dp in range(32):
            Ri[hh * 64 + dp + 32, hh * 64 + dp] = -1.0
            Ri[hh * 64 + dp, hh * 64 + dp + 32] = 1.0
    ones = np.ones((128, 128), np.float32)
    cbf = np.concatenate([ident, Ra, Ri, ones], axis=1).astype(ml_dtypes.bfloat16)
    causal = np.where(np.arange(128)[None, :] <= np.arange(128)[:, None], 0.0, -1e30).astype(np.float32)
    p = np.arange(128)
    invA = (10000.0 ** (-(np.arange(0, 128, 2, dtype=np.float32)) / 128.0)).astype(np.float32)[p % 64]
    invI = (10000.0 ** (-(np.arange(0, 64, 2, dtype=np.float32)) / 64.0)).astype(np.float32)[(p % 64) % 32]
    pow2 = np.tile((2.0 ** -(np.arange(NIT + 1) + 1.0)).astype(np.float32)[None, :], (128, 1))
    cf = np.concatenate([causal, invA[:, None], invI[:, None], pow2, np.zeros((128, 1), np.float32)], axis=1).astype(np.float32)
    return cbf, cf


def make_in_maps(x, positions, norm_g, w_in, conv_w, conv_b, w_rg, b_rg, w_ig, b_ig,
                 lru_lambda, w_out_attn, w_out_rnn, w_o, final_g, n_cores=8):
    cbf, cf = _consts()
    c = np.ascontiguousarray
    f = lambda a: c(np.asarray(a, dtype=np.float32))
    shared = {
        "norm_g": c(f(norm_g).reshape(DEPTH, 8, 128).transpose(0, 2, 1)),
        "w_in": f(w_in),
        "convw": c(f(conv_w).reshape(DEPTH, 4, NB, BD).transpose(0, 3, 2, 1).reshape(DEPTH, BD, NB * 4)),
        "chv": c(np.stack([f(conv_b), f(b_rg), f(b_ig), f(lru_lambda)], axis=1).reshape(DEPTH, 4, NB, BD)
                 .transpose(0, 3, 1, 2).reshape(DEPTH, BD, 4 * NB)),
        "w_rg": f(w_rg), "w_ig": f(w_ig),
        "w_oa": f(w_out_attn), "w_or": f(w_out_rnn), "w_o": f(w_o),
        "final_g": f(final_g).reshape(1, D),
        "cbf": cbf, "cf": cf,
    }
    maps = []
    for ci in range(n_cores):
        b = ci % 4
        m = dict(shared)
        m["x"] = f(x[b])
        m["pos"] = c(np.asarray(positions[b], dtype=np.int32).reshape(1, T))
        maps.append(m)
    return maps


_NC_CACHE = {}


def kernel(**inputs):
    if "nc" not in _NC_CACHE:
        _NC_CACHE["nc"] = K().build()
    nc = _NC_CACHE["nc"]
    maps = make_in_maps(**inputs)
    res = run_bass_kernel_spmd(nc, maps, core_ids=list(range(8)))
    out = np.stack([np.asarray(res.results[b]["out"], dtype=np.float32) for b in range(4)], axis=0)
    return out
```
